# Optimizing a Trainium2 kernel written in Bass

```python
import math
import jax, jax.numpy as jnp
from jax import lax
import numpy as np

D_MODEL = 1024
BATCH = 16
SEQ = 4096
DEPTH = 1

DIFF_HEADS = 4
DIFF_HEAD_DIM = 64
DIFF_WIDTH = DIFF_HEADS * 2 * DIFF_HEAD_DIM
FOX_HEADS = 8
FOX_HEAD_DIM = 64
FOX_WIDTH = FOX_HEADS * FOX_HEAD_DIM
MIX_WIDTH = DIFF_WIDTH + FOX_WIDTH
IN_COLS = 3 * DIFF_WIDTH + 3 * FOX_WIDTH + FOX_HEADS
D_FF = 4 * D_MODEL
N_BUCKETS = 32
MAX_DISTANCE = 128
Q_BLOCK = 128
EPS = 1e-6
NEG = -1e30

kernel_name = "hybrid_diffattn_fox_sqrelu_sandwich"


def rmsnorm(x, g):
    xf = x.astype(jnp.float32)
    y = xf * lax.rsqrt(jnp.mean(xf * xf, axis=-1, keepdims=True) + EPS)
    return (y * g.astype(jnp.float32)).astype(x.dtype)


def t5_bucket(dist):
    max_exact = N_BUCKETS // 2
    d = jnp.maximum(dist, 1).astype(jnp.float32)
    large = max_exact + (jnp.log(d / max_exact) / math.log(MAX_DISTANCE / max_exact)
                         * (N_BUCKETS - max_exact))
    large = jnp.minimum(large.astype(jnp.int32), N_BUCKETS - 1)
    return jnp.where(dist < max_exact, dist, large)


def token_mixer(h, w_in, b_f, lam_q1, lam_k1, lam_q2, lam_k2, subln_g, rel_bias, lambda_init):
    B, S, _ = h.shape
    proj = h @ w_in
    o = 0
    dq = proj[..., o:o + DIFF_WIDTH].reshape(B, S, 2 * DIFF_HEADS, DIFF_HEAD_DIM); o += DIFF_WIDTH
    dk = proj[..., o:o + DIFF_WIDTH].reshape(B, S, 2 * DIFF_HEADS, DIFF_HEAD_DIM); o += DIFF_WIDTH
    dv = proj[..., o:o + DIFF_WIDTH].reshape(B, S, DIFF_HEADS, 2 * DIFF_HEAD_DIM); o += DIFF_WIDTH
    fq = proj[..., o:o + FOX_WIDTH].reshape(B, S, FOX_HEADS, FOX_HEAD_DIM); o += FOX_WIDTH
    fk = proj[..., o:o + FOX_WIDTH].reshape(B, S, FOX_HEADS, FOX_HEAD_DIM); o += FOX_WIDTH
    fv = proj[..., o:o + FOX_WIDTH].reshape(B, S, FOX_HEADS, FOX_HEAD_DIM); o += FOX_WIDTH
    f_logit = proj[..., o:o + FOX_HEADS].astype(jnp.float32) + b_f.astype(jnp.float32)
    cum = jnp.cumsum(jax.nn.log_sigmoid(f_logit), axis=1)
    cum_k = jnp.transpose(cum, (0, 2, 1))

    lam = (jnp.exp(jnp.sum(lam_q1.astype(jnp.float32) * lam_k1.astype(jnp.float32)))
           - jnp.exp(jnp.sum(lam_q2.astype(jnp.float32) * lam_k2.astype(jnp.float32)))
           + lambda_init)

    dist_bias = rel_bias[t5_bucket(jnp.arange(S))]
    kpos = jnp.arange(S)
    nb = S // Q_BLOCK

    def to_blocks(a):
        return jnp.swapaxes(a.reshape(B, nb, Q_BLOCK, *a.shape[2:]), 0, 1)

    def block(args):
        i, qd, qf, cq = args
        qpos = i * Q_BLOCK + jnp.arange(Q_BLOCK)
        rel = qpos[:, None] - kpos[None, :]
        causal = rel >= 0
        sd = jnp.einsum('bqhd,bkhd->bhqk', qd, dk).astype(jnp.float32) * (DIFF_HEAD_DIM ** -0.5)
        bias = jnp.transpose(dist_bias[jnp.maximum(rel, 0)], (2, 0, 1)).astype(jnp.float32)
        sd = sd.reshape(B, DIFF_HEADS, 2, Q_BLOCK, S) + bias[:, None]
        pd = jax.nn.softmax(jnp.where(causal, sd, NEG), axis=-1)
        ad = pd[:, :, 0] - lam * pd[:, :, 1]
        od = jnp.einsum('bhqk,bkhe->bqhe', ad.astype(dv.dtype), dv)
        od = rmsnorm(od, subln_g) * (1.0 - lambda_init)
        sf = jnp.einsum('bqhd,bkhd->bhqk', qf, fk).astype(jnp.float32) * (FOX_HEAD_DIM ** -0.5)
        sf = sf + jnp.transpose(cq, (0, 2, 1))[..., :, None] - cum_k[..., None, :]
        pf = jax.nn.softmax(jnp.where(causal, sf, NEG), axis=-1)
        of = jnp.einsum('bhqk,bkhd->bqhd', pf.astype(fv.dtype), fv)
        return jnp.concatenate([od.reshape(B, Q_BLOCK, DIFF_WIDTH),
                                of.reshape(B, Q_BLOCK, FOX_WIDTH)], axis=-1)

    out = lax.map(block, (jnp.arange(nb), to_blocks(dq), to_blocks(fq), to_blocks(cum)))
    return jnp.swapaxes(out, 0, 1).reshape(B, S, MIX_WIDTH)


def setup_inputs(seed: int = 0) -> dict:
    key = jax.random.key(seed)
    ks = jax.random.split(key, 20)
    f32 = jnp.float32
    nrm = lambda k, shape, s: jax.random.normal(k, shape, f32) * s
    return {
        "x": jax.random.normal(ks[0], (BATCH, SEQ, D_MODEL), f32),
        "ln_attn_pre": 1.0 + nrm(ks[1], (DEPTH, D_MODEL), 0.05),
        "w_in": nrm(ks[2], (DEPTH, D_MODEL, IN_COLS), D_MODEL ** -0.5),
        "b_f": 2.0 + nrm(ks[3], (DEPTH, FOX_HEADS), 0.1),
        "lam_q1": nrm(ks[4], (DEPTH, DIFF_HEAD_DIM), 0.1),
        "lam_k1": nrm(ks[5], (DEPTH, DIFF_HEAD_DIM), 0.1),
        "lam_q2": nrm(ks[6], (DEPTH, DIFF_HEAD_DIM), 0.1),
        "lam_k2": nrm(ks[7], (DEPTH, DIFF_HEAD_DIM), 0.1),
        "subln_g": 1.0 + nrm(ks[8], (DEPTH, 2 * DIFF_HEAD_DIM), 0.05),
        "rel_bias": nrm(ks[9], (N_BUCKETS, DIFF_HEADS), 0.5),
        "w_out": nrm(ks[10], (DEPTH, MIX_WIDTH, D_MODEL), MIX_WIDTH ** -0.5),
        "ln_attn_post": 1.0 + nrm(ks[11], (DEPTH, D_MODEL), 0.05),
        "ln_mlp_pre": 1.0 + nrm(ks[12], (DEPTH, D_MODEL), 0.05),
        "w_up": nrm(ks[13], (DEPTH, D_MODEL, D_FF), D_MODEL ** -0.5),
        "w_down": nrm(ks[14], (DEPTH, D_FF, D_MODEL), D_FF ** -0.5),
        "ln_mlp_post": 1.0 + nrm(ks[15], (DEPTH, D_MODEL), 0.05),
    }


def reference(x, ln_attn_pre, w_in, b_f, lam_q1, lam_k1, lam_q2, lam_k2, subln_g, rel_bias,
              w_out, ln_attn_post, ln_mlp_pre, w_up, w_down, ln_mlp_post):
    for l in range(DEPTH):
        lambda_init = 0.8 - 0.6 * math.exp(-0.3 * l)
        h = rmsnorm(x, ln_attn_pre[l])
        mix = token_mixer(h, w_in[l], b_f[l], lam_q1[l], lam_k1[l], lam_q2[l], lam_k2[l],
                          subln_g[l], rel_bias, lambda_init)
        x = x + rmsnorm(mix @ w_out[l], ln_attn_post[l])
        h = rmsnorm(x, ln_mlp_pre[l])
        u = jnp.square(jax.nn.relu(h @ w_up[l]))
        x = x + rmsnorm(u @ w_down[l], ln_mlp_post[l])
    return x
```

```python
import math
import os
from contextlib import ExitStack

import numpy as np
import concourse.bass as bass
import concourse.mybir as mybir
from concourse.bass_utils import run_bass_kernel_spmd

F32 = mybir.dt.float32
BF16 = mybir.dt.bfloat16
AF = mybir.ActivationFunctionType
ALU = mybir.AluOpType

ENGS = ("pe", "act", "dve", "pool", "sp")

S = 4096
D = 1024
NSEQ = 2
INC = 3080
DFF = 4096
EPS = 1e-6
NEGBIG = -1e30


class Op:
    __slots__ = ("eng", "fn", "reads", "writes", "dma", "track", "idx", "deps", "marked", "semval")

    def __init__(self, eng, fn, reads, writes, dma):
        self.eng = eng
        self.fn = fn
        self.reads = reads
        self.writes = writes
        self.dma = dma
        self.deps = []
        self.marked = False


class Prog:
    def __init__(self, nc, tag):
        self.nc = nc
        self.tag = tag
        self.ops = []
        self.state = {}
        self.track_ops = {}
        self.seen = {e: {} for e in ENGS}

    def _add(self, op):
        track = op.track
        lst = self.track_ops.setdefault(track, [])
        op.idx = len(lst)
        lst.append(op)
        deps = {}
        st = self.state
        tops = self.track_ops

        def need(t, i):
            if not isinstance(t, str) and t != track:
                tops[t][i].marked = True
            if deps.get(t, -1) < i:
                deps[t] = i

        for k in op.reads:
            s = st.get(k)
            if s is not None and s[0] is not None:
                need(*s[0])
        for k in op.writes:
            s = st.get(k)
            if s is not None:
                if s[0] is not None:
                    need(*s[0])
                for t, i in s[1].items():
                    need(t, i)
        me = (track, op.idx)
        for k in op.reads:
            s = st.get(k)
            if s is None:
                s = st[k] = [None, {}]
            s[1][track] = op.idx
        for k in op.writes:
            st[k] = [me, {}]
        seen = self.seen[op.eng]
        for t, i in deps.items():
            if t == track and (op.eng == "pe" or op.dma):
                continue
            if seen.get(t, -1) >= i:
                continue
            seen[t] = i
            op.deps.append((t, i))
            self.track_ops[t][i].marked = True
        self.ops.append(op)
        return op

    def op(self, eng, fn, reads=(), writes=()):
        o = Op(eng, fn, tuple(reads), tuple(writes), False)
        o.track = eng
        return self._add(o)

    def dma(self, queue, sem, fn, reads=(), writes=()):
        o = Op(queue, fn, tuple(reads), tuple(writes), True)
        o.track = ("dma", sem)
        return self._add(o)

    def barrier(self):
        for e in ENGS:
            deps = {}
            for k, s in self.state.items():
                if s[0] is not None:
                    t, i = s[0]
                    if not isinstance(t, str):
                        self.track_ops[t][i].marked = True
                    if deps.get(t, -1) < i:
                        deps[t] = i
                for t, i in s[1].items():
                    if not isinstance(t, str):
                        self.track_ops[t][i].marked = True
                    if deps.get(t, -1) < i:
                        deps[t] = i
            o = Op(e, None, (), (), False)
            o.track = e
            lst = self.track_ops.setdefault(e, [])
            o.idx = len(lst)
            lst.append(o)
            seen = self.seen[e]
            for t, i in deps.items():
                if t == e and e == "pe":
                    continue
                if seen.get(t, -1) >= i:
                    continue
                seen[t] = i
                o.deps.append((t, i))
                self.track_ops[t][i].marked = True
            self.ops.append(o)
        self.state = {}

    def emit(self):
        nc = self.nc
        for t, lst in self.track_ops.items():
            c = 0
            for o in lst:
                if o.marked:
                    c += 1
                o.semval = c
        with ExitStack() as es:
            sems = {}
            for t in self.track_ops:
                if any(o.marked for o in self.track_ops[t]):
                    nm = self.tag + "_" + (t if isinstance(t, str) else "d_" + str(t[1]))
                    sems[t] = es.enter_context(nc.semaphore(nm))
            block = es.enter_context(nc.Block())
            per_eng = {e: [o for o in self.ops if o.eng == e] for e in ENGS}
            track_ops = self.track_ops

            def run(engobj, ename):
                for o in per_eng[ename]:
                    for (t, i) in o.deps:
                        tgt = track_ops[t][i]
                        mult = 1 if isinstance(t, str) else 16
                        engobj.wait_ge(sems[t], tgt.semval * mult)
                    if o.fn is None:
                        continue
                    ins = o.fn(engobj)
                    if o.marked:
                        ins.then_inc(sems[o.track], 16 if o.dma else 1)

            @block.tensor
            def _(e):
                run(e, "pe")

            @block.scalar
            def _(e):
                run(e, "act")

            @block.vector
            def _(e):
                run(e, "dve")

            @block.gpsimd
            def _(e):
                run(e, "pool")

            @block.sync
            def _(e):
                run(e, "sp")


def build_program(debug=False):
    nc = bass.Bass("TRN2", target_bir_lowering=False)
    dk = "ExternalOutput" if debug else "Internal"

    def din(name, shape, dt=F32):
        return nc.dram_tensor(name, shape, dt, kind="ExternalInput").ap()

    x_d = din("x", [NSEQ, S, D])
    win_d = din("w_in", [D, INC])
    wout_d = din("w_out", [D, D])
    wup_d = din("w_up", [D, DFF])
    wdn_d = din("w_down", [DFF, D])
    gpreT_d = din("g_preT", [128, D])
    gmlpT_d = din("g_mlpT", [128, D])
    gpostb_d = din("g_post_b", [128, D])
    gmpostb_d = din("g_mpost_b", [128, D])
    bf_d = din("bf", [8, 1])
    lam_d = din("lamv", [128, 256])
    subg_d = din("subg", [128, 1])
    cvec_d = din("cvec", [128, 4])
    tl_d = din("tl", [128, 5, 640])
    ident_d = din("ident", [128, 128])
    out_d = nc.dram_tensor("out", [NSEQ, S, D], F32, kind="ExternalOutput").ap()

    qk_d = nc.dram_tensor("qk_s", [NSEQ, 2048, S], BF16, kind=dk).ap()
    aux_d = nc.dram_tensor("aux_s", [NSEQ, 8, S], BF16, kind=dk).ap()
    v_d = nc.dram_tensor("v_s", [NSEQ, S, 1024], BF16, kind=dk).ap()
    mix_d = nc.dram_tensor("mix_s", [NSEQ, 1024, S], BF16, kind=dk).ap()
    wupb_d = nc.dram_tensor("wup_bf", [D, DFF], BF16, kind="Internal").ap()
    wdnb_d = nc.dram_tensor("wdn_bf", [DFF, D], BF16, kind="Internal").ap()

    with ExitStack() as es0:
        def T0(name, shape, dt):
            return es0.enter_context(nc.sbuf_tensor(name, shape, dt))

        idt = T0("idt", [128, 128], F32)
        ones_bf = T0("ones_bf", [128, 128], BF16)
        ones_f = T0("ones_f", [128, 128], F32)
        neglam = T0("neglam", [128, 1], F32)
        g08 = T0("g08", [128, 1], F32)
        negD = T0("negD", [128, NSEQ * 256], F32)
        cvt = T0("cvt", [128, 4], F32)
        negbf = T0("negbf", [8, 1], F32)

        with ExitStack() as es:
            def T(name, shape, dt):
                return es.enter_context(nc.sbuf_tensor(name, shape, dt))
            lamt = T("lamt", [128, 256], F32)
            lprod = T("lprod", [128, 128], F32)
            lsum = T("lsum", [128, 2], F32)
            lexp = T("lexp", [128, 2], F32)
            subg = T("subg_t", [128, 1], F32)
            bft = T("bft", [8, 1], F32)
            pg = Prog(nc, "p0")
            pg.dma("sp", "a", lambda e: e.dma_start(out=idt[:], in_=ident_d), writes=["idt"])
            pg.dma("sp", "b", lambda e: e.dma_start(out=lamt[:], in_=lam_d), writes=["lamt"])
            pg.dma("sp", "c", lambda e: e.dma_start(out=subg[:], in_=subg_d), writes=["subg"])
            pg.dma("sp", "d", lambda e: e.dma_start(out=cvt[:], in_=cvec_d), writes=["cvt"])
            pg.dma("sp", "e", lambda e: e.dma_start(out=bft[:], in_=bf_d), writes=["bft"])
            pg.op("dve", lambda e: e.memset(ones_bf[:], 1.0), writes=["ones_bf"])
            pg.op("dve", lambda e: e.memset(ones_f[:], 1.0), writes=["ones_f"])
            pg.op("dve", lambda e: e.tensor_tensor(out=lprod[:, 0:64], in0=lamt[:, 0:64], in1=lamt[:, 64:128], op=ALU.mult),
                  reads=["lamt"], writes=["lp0"])
            pg.op("dve", lambda e: e.tensor_tensor(out=lprod[:, 64:128], in0=lamt[:, 128:192], in1=lamt[:, 192:256], op=ALU.mult),
                  reads=["lamt"], writes=["lp1"])
            pg.op("dve", lambda e: e.reduce_sum(out=lsum[:, 0:1], in_=lprod[:, 0:64], axis=mybir.AxisListType.X),
                  reads=["lp0"], writes=["ls0"])
            pg.op("dve", lambda e: e.reduce_sum(out=lsum[:, 1:2], in_=lprod[:, 64:128], axis=mybir.AxisListType.X),
                  reads=["lp1"], writes=["ls1"])
            pg.op("act", lambda e: e.activation(out=lexp[:], in_=lsum[:], func=AF.Exp), reads=["ls0", "ls1"], writes=["lexp"])
            pg.op("dve", lambda e: e.tensor_tensor(out=neglam[:], in0=lexp[:, 1:2], in1=lexp[:, 0:1], op=ALU.subtract),
                  reads=["lexp"], writes=["neglam"])
            pg.op("dve", lambda e: e.tensor_scalar(out=neglam[:], in0=neglam[:], scalar1=-0.2, scalar2=None, op0=ALU.add),
                  reads=["neglam"], writes=["neglam"])
            pg.op("dve", lambda e: e.tensor_scalar(out=g08[:], in0=subg[:], scalar1=0.8, scalar2=None, op0=ALU.mult),
                  reads=["subg"], writes=["g08"])
            pg.op("dve", lambda e: e.tensor_scalar(out=negbf[:], in0=bft[:], scalar1=-1.0, scalar2=None, op0=ALU.mult),
                  reads=["bft"], writes=["negbf"])
            pg.barrier()
            pg.emit()

        with ExitStack() as es:
            def T(name, shape, dt):
                return es.enter_context(nc.sbuf_tensor(name, shape, dt))

            def PS(name, shape, dt=F32):
                return es.enter_context(nc.psum_tensor(name, shape, dt))
            winb = T("winb", [128, 8, INC], BF16)
            gT = T("gT", [128, 8, 128], F32)
            xt = [T(f"xt{i}", [128, D], F32) for i in range(2)]
            xs = [T(f"xs{i}", [128, D], F32) for i in range(2)]
            sqj = T("sqj", [128, D], BF16)
            ss = [T(f"ss{i}", [128, 1], F32) for i in range(2)]
            lnv = [T(f"lnv{i}", [128, 1], F32) for i in range(2)]
            rs = [T(f"rs{i}", [128, 1], F32) for i in range(2)]
            xnT = [T(f"xnT{i}", [128, 8, 512], BF16) for i in range(2)]
            stQK = [T(f"stQK{i}", [128, 16, 512], BF16) for i in range(2)]
            stV = [T(f"stV{i}", [128, 4, 1024], BF16) for i in range(2)]
            e8 = T("e8", [8, 512], F32)
            sp8 = T("sp8", [8, 512], F32)
            ones8 = T("ones8", [8, 512], F32)
            cumT = T("cumT", [8, S], F32)
            cumb = T("cumb", [8, S], BF16)
            tp = [[PS(f"tp{i}{h}", [128, 512]) for h in range(2)] for i in range(2)]
            mm = [PS(f"mm{i}", [128, 512]) for i in range(3)]

            pg = Prog(nc, "pA")
            win_v = win_d.rearrange("(kc p) c -> p kc c", p=128)
            pg.dma("pool", "win0", lambda e: e.dma_start(out=winb[:, :, 0:1540], in_=win_v[:, :, 0:1540]), writes=["winb0"])
            pg.dma("pool", "win1", lambda e: e.dma_start(out=winb[:, :, 1540:INC], in_=win_v[:, :, 1540:INC]), writes=["winb1"])
            pg.dma("sp", "gT", lambda e: e.dma_start(out=gT[:], in_=gpreT_d.rearrange("p (k j) -> p k j", j=128)), writes=["gT"])
            pg.op("dve", lambda e: e.memset(ones8[:], 1.0), writes=["ones8"])
            wup_v = wup_d.rearrange("r (a c) -> (r a) c", c=2048)
            wupb_v = wupb_d.rearrange("r (a c) -> (r a) c", c=2048)
            for q in range(4):
                pg.dma("pool", f"wupc{q}", lambda e, q=q: e.dma_start(out=wupb_v[q * 512:(q + 1) * 512, :], in_=wup_v[q * 512:(q + 1) * 512, :]),
                       writes=[("wupb", q)])
            wdn_v = wdn_d.rearrange("(r a) c -> r (a c)", a=2)
            wdnb_v = wdnb_d.rearrange("(r a) c -> r (a c)", a=2)
            for q in range(4):
                pg.dma("pool", f"wdnc{q}", lambda e, q=q: e.dma_start(out=wdnb_v[q * 512:(q + 1) * 512, :], in_=wdn_v[q * 512:(q + 1) * 512, :]),
                       writes=[("wdnb", q)])

            def qk_cols(oc):
                if oc < 4:
                    return oc * 128, 0.125
                if oc < 8:
                    return 512 + (oc - 4) * 128, 1.0
                if oc < 12:
                    return 1536 + (oc - 8) * 128, 0.125
                return 2048 + (oc - 12) * 128, 1.0

            mmc = [0]

            def next_mm():
                b = mmc[0] % 3
                mmc[0] += 1
                return b

            evc = [0]
            for s in range(NSEQ):
                for tb in range(8):
                    blk = s * 8 + tb
                    xn = xnT[blk % 2]
                    xnk = ("xnT", blk % 2)
                    for tt in range(4):
                        ti = blk * 4 + tt
                        sl = ti % 2
                        r0 = tb * 512 + tt * 128
                        pg.dma("sp", f"x{sl}", lambda e, sl=sl, s=s, r0=r0: e.dma_start(out=xt[sl][:], in_=x_d[s, r0:r0 + 128, :]),
                               writes=[("xt", sl)])
                        pg.op("act", lambda e, sl=sl: e.activation(out=sqj[:], in_=xt[sl][:], func=AF.Square, accum_out=ss[sl][:]),
                              reads=[("xt", sl)], writes=[("ss", sl)])
                        pg.op("act", lambda e, sl=sl: e.activation(out=lnv[sl][:], in_=ss[sl][:], func=AF.Ln, scale=1.0 / D, bias=EPS),
                              reads=[("ss", sl)], writes=[("lnv", sl)])
                        pg.op("act", lambda e, sl=sl: e.activation(out=rs[sl][:], in_=lnv[sl][:], func=AF.Exp, scale=-0.5),
                              reads=[("lnv", sl)], writes=[("rs", sl)])
                        pg.op("act", lambda e, sl=sl: e.activation(out=xs[sl][:], in_=xt[sl][:], func=AF.Copy, scale=rs[sl][:]),
                              reads=[("xt", sl), ("rs", sl)], writes=[("xs", sl)])
                        for half in range(2):
                            for j in range(4):
                                kc = half * 4 + j
                                pg.op("pe", lambda e, sl=sl, half=half, j=j, kc=kc: e.transpose(
                                    out=tp[sl][half][:, j * 128:(j + 1) * 128], in_=xs[sl][:, kc * 128:(kc + 1) * 128], identity=idt[:]),
                                    reads=[("xs", sl), "idt"], writes=[("tp", sl, half)])
                            pg.op("dve", lambda e, sl=sl, half=half, xn=xn, tt=tt: e.tensor_tensor(
                                out=xn[:, half * 4:(half + 1) * 4, tt * 128:(tt + 1) * 128],
                                in0=tp[sl][half][:].rearrange("p (k j) -> p k j", j=128),
                                in1=gT[:, half * 4:(half + 1) * 4, :], op=ALU.mult),
                                reads=[("tp", sl, half), "gT"], writes=[xnk])
                    stq = stQK[blk % 2]
                    for oc in range(16):
                        c0, scl = qk_cols(oc)
                        b = next_mm()
                        for kc in range(8):
                            pg.op("pe", lambda e, b=b, kc=kc, c0=c0, xn=xn: e.matmul(
                                mm[b][:], lhsT=winb[:, kc, c0:c0 + 128], rhs=xn[:, kc, :], start=(kc == 0), stop=(kc == 7)),
                                reads=[xnk, "winb0", "winb1"], writes=[("mm", b)])
                        evc[0] += 1
                        if evc[0] % 2 == 0:
                            pg.op("act", lambda e, b=b, oc=oc, scl=scl, stq=stq: e.activation(
                                out=stq[:, oc, :], in_=mm[b][:], func=AF.Copy, scale=scl),
                                reads=[("mm", b)], writes=[("stQK", blk % 2)])
                        else:
                            pg.op("dve", lambda e, b=b, oc=oc, scl=scl, stq=stq: e.tensor_scalar(
                                out=stq[:, oc, :], in0=mm[b][:], scalar1=scl, scalar2=None, op0=ALU.mult),
                                reads=[("mm", b)], writes=[("stQK", blk % 2)])
                    pg.dma("pool", f"sqk{blk % 2}", lambda e, s=s, tb=tb, stq=stq: e.dma_start(
                        out=qk_d[s].rearrange("(oc p) t -> p oc t", p=128)[:, :, tb * 512:(tb + 1) * 512], in_=stq[:]),
                        reads=[("stQK", blk % 2)], writes=[("qk_d", s, tb)])
                    stv = stV[blk % 2]
                    for tt in range(4):
                        for half in range(2):
                            vc0 = 1024 if half == 0 else 2560
                            b = next_mm()
                            for kc in range(8):
                                pg.op("pe", lambda e, b=b, kc=kc, vc0=vc0, xn=xn, tt=tt: e.matmul(
                                    mm[b][:], lhsT=xn[:, kc, tt * 128:(tt + 1) * 128], rhs=winb[:, kc, vc0:vc0 + 512],
                                    start=(kc == 0), stop=(kc == 7)),
                                    reads=[xnk, "winb0", "winb1"], writes=[("mm", b)])
                            evc[0] += 1
                            if evc[0] % 2 == 0:
                                pg.op("act", lambda e, b=b, tt=tt, half=half, stv=stv: e.activation(
                                    out=stv[:, tt, half * 512:(half + 1) * 512], in_=mm[b][:], func=AF.Copy),
                                    reads=[("mm", b)], writes=[("stV", blk % 2)])
                            else:
                                pg.op("dve", lambda e, b=b, tt=tt, half=half, stv=stv: e.tensor_copy(
                                    out=stv[:, tt, half * 512:(half + 1) * 512], in_=mm[b][:]),
                                    reads=[("mm", b)], writes=[("stV", blk % 2)])
                    pg.dma("pool", f"sv{blk % 2}", lambda e, s=s, tb=tb, stv=stv: e.dma_start(
                        out=v_d[s, tb * 512:(tb + 1) * 512, :].rearrange("(tt p) c -> p tt c", p=128), in_=stv[:]),
                        reads=[("stV", blk % 2)], writes=[("v_d", s, tb)])
                    b = next_mm()
                    for kc in range(8):
                        pg.op("pe", lambda e, b=b, kc=kc, xn=xn: e.matmul(
                            mm[b][0:8, :], lhsT=winb[:, kc, 3072:3080], rhs=xn[:, kc, :], start=(kc == 0), stop=(kc == 7)),
                            reads=[xnk, "winb0", "winb1"], writes=[("mm", b)])
                    pg.op("act", lambda e, b=b: e.activation(out=e8[:], in_=mm[b][0:8, :], func=AF.Exp, scale=-1.0, bias=negbf[:]),
                          reads=[("mm", b), "negbf"], writes=["e8"])
                    pg.op("act", lambda e: e.activation(out=sp8[:], in_=e8[:], func=AF.Ln, bias=1.0),
                          reads=["e8"], writes=["sp8"])
                    if tb == 0:
                        pg.op("dve", lambda e: e.tensor_tensor_scan(out=cumT[:, 0:512], data0=ones8[:], data1=sp8[:], initial=0.0,
                                                                    op0=ALU.mult, op1=ALU.subtract),
                              reads=["sp8", "ones8"], writes=["cumT"])
                    else:
                        pg.op("dve", lambda e, tb=tb: e.tensor_tensor_scan(
                            out=cumT[:, tb * 512:(tb + 1) * 512], data0=ones8[:], data1=sp8[:],
                            initial=cumT[:, tb * 512 - 1:tb * 512], op0=ALU.mult, op1=ALU.subtract),
                            reads=["sp8", "ones8", "cumT"], writes=["cumT"])
                pg.op("dve", lambda e: e.tensor_copy(out=cumb[:], in_=cumT[:]), reads=["cumT"], writes=["cumb"])
                pg.dma("sp", "aux", lambda e, s=s: e.dma_start(out=aux_d[s], in_=cumb[:]), reads=["cumb"], writes=[("aux_d", s)])
                b = next_mm()
                for kb in range(32):
                    pg.op("pe", lambda e, b=b, kb=kb: e.transpose(
                        out=mm[b][:, kb * 8:(kb + 1) * 8], in_=cumT[0:8, kb * 128:(kb + 1) * 128], identity=idt[0:8, 0:8]),
                        reads=["cumT", "idt"], writes=[("mm", b)])
                pg.op("dve", lambda e, b=b, s=s: e.tensor_scalar(
                    out=negD[:, s * 256:(s + 1) * 256], in0=mm[b][:, 0:256], scalar1=-1.0, scalar2=None, op0=ALU.mult),
                    reads=[("mm", b)], writes=["negD"])
            pg.barrier()
            pg.emit()

        with ExitStack() as es:
            def T(name, shape, dt):
                return es.enter_context(nc.sbuf_tensor(name, shape, dt))

            def PS(name, shape, dt=F32):
                return es.enter_context(nc.psum_tensor(name, shape, dt))
            QT = [T(f"QT{i}", [128, S], BF16) for i in range(2)]
            KT = [T(f"KT{i}", [128, S], BF16) for i in range(2)]
            VT = [T(f"VT{i}", [128, 32, 128], BF16) for i in range(2)]
            TL = T("TL", [128, 5, 640], F32)
            NPT = 4
            PT = [T(f"PT{i}", [128, 512], BF16) for i in range(NPT)]
            tmpb = [T(f"tmpb{i}", [128, 512], F32) for i in range(2)]
            mix = [T(f"mix{i}", [128, S], BF16) for i in range(2)]
            R0 = T("R0", [128, 512], F32)
            R1 = T("R1", [128, 512], F32)
            Tt0 = T("Tt0", [128, 512], F32)
            Tt1 = T("Tt1", [128, 512], F32)
            od = T("od", [128, 512], F32)
            sqd = T("sqd", [128, 512], F32)
            lnd = T("lnd", [128, 512], F32)
            rsd = T("rsd", [128, 512], F32)
            NSB = 4
            Sb = [PS(f"Sb{i}", [128, 512]) for i in range(NSB)]
            Ob = [PS(f"Ob{i}", [128, 512]) for i in range(2)]
            Lb = [PS(f"Lb{i}", [128, 512]) for i in range(2)]

            pg = Prog(nc, "pB")
            pg.dma("sp", "tl", lambda e: e.dma_start(out=TL[:], in_=tl_d), writes=["TL"])

            units = []
            for s in range(NSEQ):
                for h in range(4):
                    units.append((s, "d", h))
                for h in range(8):
                    units.append((s, "f", h))

            def load_unit(ui):
                s, kind, h = units[ui]
                sl = ui % 2
                if kind == "d":
                    pg.dma("sp", f"q{sl}", lambda e: e.dma_start(out=QT[sl][:], in_=qk_d[s, h * 128:(h + 1) * 128, :]),
                           writes=[("Q", sl)])
                    pg.dma("sp", f"k{sl}", lambda e: e.dma_start(out=KT[sl][:], in_=qk_d[s, (4 + h) * 128:(5 + h) * 128, :]),
                           writes=[("K", sl)])
                    pg.dma("sp", f"v{sl}", lambda e: e.dma_start(
                        out=VT[sl][:], in_=v_d[s, :, h * 128:(h + 1) * 128].rearrange("(kb p) c -> p kb c", p=128)),
                        writes=[("V", sl)])
                else:
                    qr = (8 + h // 2) * 128 + (h % 2) * 64
                    kr = (12 + h // 2) * 128 + (h % 2) * 64
                    pg.op("pool", lambda e: e.memset(KT[sl][64:65, :], 1.0), writes=[("K", sl)])
                    pg.dma("sp", f"q{sl}", lambda e: e.dma_start(out=QT[sl][0:64, :], in_=qk_d[s, qr:qr + 64, :]),
                           writes=[("Q", sl)])
                    pg.dma("sp", f"q{sl}", lambda e: e.dma_start(out=QT[sl][64:65, :], in_=aux_d[s, h:h + 1, :]),
                           writes=[("Qa", sl)])
                    pg.dma("sp", f"k{sl}", lambda e: e.dma_start(out=KT[sl][0:64, :], in_=qk_d[s, kr:kr + 64, :]),
                           reads=[("K", sl)], writes=[("Kd", sl)])
                    pg.dma("sp", f"v{sl}", lambda e: e.dma_start(
                        out=VT[sl][:, :, 0:64], in_=v_d[s, :, 512 + h * 64:512 + (h + 1) * 64].rearrange("(kb p) c -> p kb c", p=128)),
                        writes=[("V", sl)])

            sctr = [0]
            pctr = [0]
            tctr = [0]

            def run_unit(ui):
                s, kind, h = units[ui]
                sl = ui % 2
                nstream = 2 if kind == "d" else 1
                if kind == "d":
                    chunk = h
                    po = 0
                    dv = 128
                    tli = h
                else:
                    chunk = 4 + h // 2
                    po = (h % 2) * 64
                    dv = 64
                    tli = 4
                csl = chunk % 2
                qkeys = [("Q", sl), ("Qa", sl)]
                kkeys = [("K", sl), ("Kd", sl)]
                blocks = []
                for qb in range(8):
                    nkb = 4 * qb + 4
                    for kb in range(nkb):
                        for i in range(nstream):
                            blocks.append((qb, kb, i, kb == nkb - 1 and i == nstream - 1))
                info = {}

                def emit_qk(n):
                    qb, kb, i, _ = blocks[n]
                    off = qb * 512 - kb * 128
                    c0 = max(0, -off)
                    sb = sctr[0] % NSB
                    sctr[0] += 1
                    info[n] = (sb, c0, off)
                    if kind == "d":
                        pr = slice(64 * i, 64 * i + 64)
                    else:
                        pr = slice(0, 65)
                    pg.op("pe", lambda e: e.matmul(
                        Sb[sb][:, c0:512], lhsT=KT[sl][pr, kb * 128:(kb + 1) * 128],
                        rhs=QT[sl][pr, qb * 512 + c0:(qb + 1) * 512], start=True, stop=True),
                        reads=qkeys + kkeys, writes=[("S", sb)])

                def emit_rest(n):
                    qb, kb, i, lastq = blocks[n]
                    sb, c0, off = info.pop(n)
                    nkb = 4 * qb + 4
                    special = (off <= 128) if kind == "d" else (off <= 0)
                    ps = pctr[0] % NPT
                    pctr[0] += 1
                    if kind == "d":
                        bias_ap = None if special else cvt[:, h:h + 1]
                    else:
                        ix = (s * 32 + kb) * 8 + h
                        bias_ap = negD[:, ix:ix + 1]
                    if special:
                        ts = tctr[0] % 2
                        tctr[0] += 1
                        t0 = max(off, 0)
                        pg.op("dve", lambda e: e.tensor_tensor(
                            out=tmpb[ts][:, c0:512], in0=Sb[sb][:, c0:512], in1=TL[:, tli, t0:t0 + 512 - c0], op=ALU.add),
                            reads=[("S", sb), "TL"], writes=[("tmp", ts)])
                        src = tmpb[ts]
                        rk = ("tmp", ts)
                    else:
                        src = Sb[sb]
                        rk = ("S", sb)
                    if bias_ap is None:
                        pg.op("act", lambda e: e.activation(out=PT[ps][:, c0:512], in_=src[:, c0:512], func=AF.Exp),
                              reads=[rk], writes=[("PT", ps)])
                    else:
                        pg.op("act", lambda e: e.activation(out=PT[ps][:, c0:512], in_=src[:, c0:512], func=AF.Exp, bias=bias_ap),
                              reads=[rk, "negD", "cvt"], writes=[("PT", ps)])
                    pg.op("pe", lambda e: e.matmul(
                        Ob[i][po:po + dv, c0:512], lhsT=VT[sl][:, kb, 0:dv], rhs=PT[ps][:, c0:512],
                        start=(kb == 0), stop=(kb == nkb - 1)),
                        reads=[("V", sl), ("PT", ps)], writes=[("O", i)])
                    pg.op("pe", lambda e: e.matmul(
                        Lb[i][po:po + dv, c0:512], lhsT=ones_bf[:, 0:dv], rhs=PT[ps][:, c0:512],
                        start=(kb == 0), stop=(kb == nkb - 1)),
                        reads=["ones_bf", ("PT", ps)], writes=[("L", i)])
                    if lastq:
                        finalize(qb)

                def finalize(qb):
                    cs = slice(qb * 512, (qb + 1) * 512)
                    mk = ("mix", csl)
                    if kind == "d":
                        pg.op("dve", lambda e: e.reciprocal(out=R0[:], in_=Lb[0][:]), reads=[("L", 0)], writes=["R0"])
                        pg.op("dve", lambda e: e.tensor_tensor(out=Tt0[:], in0=Ob[0][:], in1=R0[:], op=ALU.mult),
                              reads=[("O", 0), "R0"], writes=["T0"])
                        pg.op("dve", lambda e: e.reciprocal(out=R1[:], in_=Lb[1][:]), reads=[("L", 1)], writes=["R1"])
                        pg.op("dve", lambda e: e.tensor_tensor(out=Tt1[:], in0=Ob[1][:], in1=R1[:], op=ALU.mult),
                              reads=[("O", 1), "R1"], writes=["T1"])
                        pg.op("dve", lambda e: e.scalar_tensor_tensor(
                            out=od[:], in0=Tt1[:], scalar=neglam[:, 0:1], in1=Tt0[:], op0=ALU.mult, op1=ALU.add),
                            reads=["T0", "T1"], writes=["od"])
                        pg.op("pool", lambda e: e.tensor_tensor(out=sqd[:], in0=od[:], in1=od[:], op=ALU.mult),
                              reads=["od"], writes=["sqd"])
                        sb = sctr[0] % NSB
                        sctr[0] += 1
                        pg.op("pe", lambda e: e.matmul(Sb[sb][:], lhsT=ones_f[:], rhs=sqd[:], start=True, stop=True),
                              reads=["sqd", "ones_f"], writes=[("S", sb)])
                        pg.op("act", lambda e: e.activation(out=lnd[:], in_=Sb[sb][:], func=AF.Ln, scale=1.0 / 128, bias=EPS),
                              reads=[("S", sb)], writes=["lnd"])
                        pg.op("act", lambda e: e.activation(out=rsd[:], in_=lnd[:], func=AF.Exp, scale=-0.5),
                              reads=["lnd"], writes=["rsd"])
                        pg.op("dve", lambda e: e.scalar_tensor_tensor(
                            out=mix[csl][:, cs], in0=od[:], scalar=g08[:, 0:1], in1=rsd[:], op0=ALU.mult, op1=ALU.mult),
                            reads=["od", "rsd"], writes=[mk])
                    else:
                        pp = slice(po, po + 64)
                        pg.op("dve", lambda e: e.reciprocal(out=R0[pp, :], in_=Lb[0][pp, :]), reads=[("L", 0)], writes=["R0"])
                        pg.op("dve", lambda e: e.tensor_tensor(out=mix[csl][pp, cs], in0=Ob[0][pp, :], in1=R0[pp, :], op=ALU.mult),
                              reads=[("O", 0), "R0"], writes=[mk])

                LA = 2
                nb = len(blocks)
                for n in range(min(LA, nb)):
                    emit_qk(n)
                for n in range(nb):
                    if n + LA < nb:
                        emit_qk(n + LA)
                    emit_rest(n)
                if kind == "d" or (h % 2 == 1):
                    pg.dma("sp", f"mx{csl}", lambda e: e.dma_start(out=mix_d[s, chunk * 128:(chunk + 1) * 128, :], in_=mix[csl][:]),
                           reads=[("mix", csl)], writes=[("mix_d", s, chunk)])

            load_unit(0)
            for ui in range(len(units)):
                if ui + 1 < len(units):
                    load_unit(ui + 1)
                run_unit(ui)
            pg.barrier()
            pg.emit()

        with ExitStack() as es:
            def T(name, shape, dt):
                return es.enter_context(nc.sbuf_tensor(name, shape, dt))

            def PS(name, shape, dt=F32):
                return es.enter_context(nc.psum_tensor(name, shape, dt))
            woutb = T("woutb", [128, 8, D], BF16)
            gpostb = T("gpostb", [128, D], F32)
            gmpostb = T("gmpostb", [128, D], F32)
            g2T = T("g2T", [128, 8, 128], F32)
            mixb = [T(f"mixb{i}", [128, 8, 512], BF16) for i in range(2)]
            xt = [T(f"cxt{i}", [128, D], F32) for i in range(2)]
            x1 = T("x1", [128, 4, D], F32)
            xs2 = [T(f"xs2{i}", [128, D], F32) for i in range(2)]
            sqj = T("csqj", [128, D], BF16)
            h2T = [T(f"h2T{i}", [128, 8, 512], BF16) for i in range(2)]
            wup = [T(f"wup{i}", [128, 8, 512], BF16) for i in range(3)]
            wdn = [T(f"wdn{i}", [128, 4, D], BF16) for i in range(3)]
            rl = [T(f"rl{i}", [128, 512], F32) for i in range(2)]
            uT = [T(f"uT{i}", [128, 4, 512], BF16) for i in range(2)]
            acc = T("acc", [128, 4, D], F32)
            tmpn = T("tmpn", [128, D], F32)
            outst = [T(f"outst{i}", [128, D], F32) for i in range(2)]
            ssc = [T(f"ssc{i}", [128, 1], F32) for i in range(6)]
            lnc = [T(f"lnc{i}", [128, 1], F32) for i in range(6)]
            rsc = [T(f"rsc{i}", [128, 1], F32) for i in range(6)]
            Y = PS("Y", [128, 1024])
            TP = [PS(f"TP{i}", [128, 512]) for i in range(2)]
            U = [PS(f"U{i}", [128, 512]) for i in range(2)]
            Yd = [PS(f"Yd{i}", [128, 512]) for i in range(2)]

            pg = Prog(nc, "pC")
            pg.dma("pool", "wo", lambda e: e.dma_start(out=woutb[:], in_=wout_d.rearrange("(kc p) c -> p kc c", p=128)), writes=["woutb"])
            pg.dma("sp", "g1", lambda e: e.dma_start(out=gpostb[:], in_=gpostb_d), writes=["gpostb"])
            pg.dma("sp", "g2", lambda e: e.dma_start(out=gmpostb[:], in_=gmpostb_d), writes=["gmpostb"])
            pg.dma("sp", "g3", lambda e: e.dma_start(out=g2T[:], in_=gmlpT_d.rearrange("p (k j) -> p k j", j=128)), writes=["g2T"])

            nblk = NSEQ * 8
            nrm = [0]

            def rstd_from(src_ap, srckeys, n_el):
                k = nrm[0] % 6
                nrm[0] += 1
                pg.op("act", lambda e: e.activation(out=sqj[:], in_=src_ap, func=AF.Square, accum_out=ssc[k][:]),
                      reads=srckeys, writes=[("ssc", k)])
                pg.op("act", lambda e: e.activation(out=lnc[k][:], in_=ssc[k][:], func=AF.Ln, scale=1.0 / n_el, bias=EPS),
                      reads=[("ssc", k)], writes=[("lnc", k)])
                pg.op("act", lambda e: e.activation(out=rsc[k][:], in_=lnc[k][:], func=AF.Exp, scale=-0.5),
                      reads=[("lnc", k)], writes=[("rsc", k)])
                return rsc[k], ("rsc", k)

            NW = 3
            witems = [(blk, fg) for blk in range(nblk) for fg in range(8)]
            wnext = [0]

            def ensure_w(upto):
                while wnext[0] <= upto and wnext[0] < len(witems):
                    load_w(wnext[0])
                    wnext[0] += 1

            def load_w(item):
                blk, fg = witems[item]
                i = item % NW
                pg.dma("sp", f"wu{i}", lambda e: e.dma_start(
                    out=wup[i][:], in_=wupb_d.rearrange("(kc p) c -> p kc c", p=128)[:, :, fg * 512:(fg + 1) * 512]),
                    reads=[("wupb", q) for q in range(4)], writes=[("wup", i)])
                pg.dma("sp", f"wd{i}", lambda e: e.dma_start(
                    out=wdn[i][:], in_=wdnb_d[fg * 512:(fg + 1) * 512, :].rearrange("(fc p) c -> p fc c", p=128)),
                    reads=[("wdnb", q) for q in range(4)], writes=[("wdn", i)])

            def load_mix(blk):
                s, tb = divmod(blk, 8)
                i = blk % 2
                pg.dma("sp", f"mb{i}", lambda e: e.dma_start(
                    out=mixb[i][:], in_=mix_d[s].rearrange("(kc p) t -> p kc t", p=128)[:, :, tb * 512:(tb + 1) * 512]),
                    writes=[("mixb", i)])

            xl = [0]
            ost = [0]

            def c_tile(blk, tt):
                s, tb = divmod(blk, 8)
                mb = mixb[blk % 2]
                hT = h2T[blk % 2]
                hk = ("h2T", blk % 2)
                r0 = tb * 512 + tt * 128
                xi = xl[0] % 2
                xl[0] += 1
                pg.dma("sp", f"cx{xi}", lambda e: e.dma_start(out=xt[xi][:], in_=x_d[s, r0:r0 + 128, :]),
                       writes=[("xt", xi)])
                for half in range(2):
                    for kc in range(8):
                        pg.op("pe", lambda e, half=half, kc=kc: e.matmul(
                            Y[:, half * 512:(half + 1) * 512], lhsT=mb[:, kc, tt * 128:(tt + 1) * 128],
                            rhs=woutb[:, kc, half * 512:(half + 1) * 512], start=(kc == 0), stop=(kc == 7)),
                            reads=[("mixb", blk % 2), "woutb"], writes=["Y"])
                rt, rk = rstd_from(Y[:], ["Y"], D)
                pg.op("dve", lambda e: e.scalar_tensor_tensor(
                    out=tmpn[:], in0=Y[:], scalar=rt[:, 0:1], in1=gpostb[:], op0=ALU.mult, op1=ALU.mult),
                    reads=["Y", rk, "gpostb"], writes=["tmpn"])
                pg.op("dve", lambda e: e.tensor_tensor(out=x1[:, tt, :], in0=tmpn[:], in1=xt[xi][:], op=ALU.add),
                      reads=["tmpn", ("xt", xi)], writes=[("x1", tt)])
                rt2, rk2 = rstd_from(x1[:, tt, :], [("x1", tt)], D)
                xsi = xi
                pg.op("act", lambda e: e.activation(out=xs2[xsi][:], in_=x1[:, tt, :], func=AF.Copy, scale=rt2[:]),
                      reads=[("x1", tt), rk2], writes=[("xs2", xsi)])
                for half in range(2):
                    for j in range(4):
                        kc = half * 4 + j
                        pg.op("pe", lambda e, half=half, j=j, kc=kc: e.transpose(
                            out=TP[half][:, j * 128:(j + 1) * 128], in_=xs2[xsi][:, kc * 128:(kc + 1) * 128], identity=idt[:]),
                            reads=[("xs2", xsi), "idt"], writes=[("TP", half)])
                    pg.op("dve", lambda e, half=half: e.tensor_tensor(
                        out=hT[:, half * 4:(half + 1) * 4, tt * 128:(tt + 1) * 128],
                        in0=TP[half][:].rearrange("p (k j) -> p k j", j=128),
                        in1=g2T[:, half * 4:(half + 1) * 4, :], op=ALU.mult),
                        reads=[("TP", half), "g2T"], writes=[hk])

            def up(blk, fg):
                hT = h2T[blk % 2]
                hk = ("h2T", blk % 2)
                wi = (blk * 8 + fg) % NW
                ui_ = fg % 2
                for fcl in range(4):
                    ub = (fg * 4 + fcl) % 2
                    for kc in range(8):
                        pg.op("pe", lambda e, ub=ub, kc=kc, fcl=fcl: e.matmul(
                            U[ub][:], lhsT=wup[wi][:, kc, fcl * 128:(fcl + 1) * 128], rhs=hT[:, kc, :],
                            start=(kc == 0), stop=(kc == 7)),
                            reads=[("wup", wi), hk], writes=[("U", ub)])
                    pg.op("act", lambda e, ub=ub: e.activation(out=rl[ub][:], in_=U[ub][:], func=AF.Relu),
                          reads=[("U", ub)], writes=[("rl", ub)])
                    pg.op("pool", lambda e, ub=ub, fcl=fcl: e.tensor_tensor(
                        out=uT[ui_][:, fcl, :], in0=rl[ub][:], in1=rl[ub][:], op=ALU.mult),
                        reads=[("rl", ub)], writes=[("uT", ui_)])

            def down(blk, fg):
                wi = (blk * 8 + fg) % NW
                ui_ = fg % 2
                for tt in range(4):
                    for half in range(2):
                        yb = half
                        for fcl in range(4):
                            pg.op("pe", lambda e, yb=yb, fcl=fcl, tt=tt, half=half: e.matmul(
                                Yd[yb][:], lhsT=uT[ui_][:, fcl, tt * 128:(tt + 1) * 128],
                                rhs=wdn[wi][:, fcl, half * 512:(half + 1) * 512], start=(fcl == 0), stop=(fcl == 3)),
                                reads=[("uT", ui_), ("wdn", wi)], writes=[("Yd", yb)])
                        ak = ("acc", tt, half)
                        if fg == 0:
                            pg.op("dve", lambda e, yb=yb, tt=tt, half=half: e.tensor_copy(
                                out=acc[:, tt, half * 512:(half + 1) * 512], in_=Yd[yb][:]),
                                reads=[("Yd", yb)], writes=[ak])
                        else:
                            pg.op("dve", lambda e, yb=yb, tt=tt, half=half: e.tensor_tensor(
                                out=acc[:, tt, half * 512:(half + 1) * 512], in0=Yd[yb][:],
                                in1=acc[:, tt, half * 512:(half + 1) * 512], op=ALU.add),
                                reads=[("Yd", yb), ak], writes=[ak])

            def final_tile(blk, tt):
                s, tb = divmod(blk, 8)
                r0 = tb * 512 + tt * 128
                rt3, rk3 = rstd_from(acc[:, tt, :], [("acc", tt, 0), ("acc", tt, 1)], D)
                oi = ost[0] % 2
                ost[0] += 1
                pg.op("dve", lambda e: e.scalar_tensor_tensor(
                    out=outst[oi][:], in0=acc[:, tt, :], scalar=rt3[:, 0:1], in1=gmpostb[:], op0=ALU.mult, op1=ALU.mult),
                    reads=[("acc", tt, 0), ("acc", tt, 1), rk3, "gmpostb"], writes=[("ost", oi)])
                pg.op("pool", lambda e: e.tensor_tensor(out=outst[oi][:], in0=outst[oi][:], in1=x1[:, tt, :], op=ALU.add),
                      reads=[("ost", oi), ("x1", tt)], writes=[("ost", oi)])
                pg.dma("pool", f"o{oi}", lambda e: e.dma_start(out=out_d[s, r0:r0 + 128, :], in_=outst[oi][:]),
                       reads=[("ost", oi)], writes=[("out", blk, tt)])

            load_mix(0)
            ensure_w(1)
            for blk in range(nblk):
                if blk + 1 < nblk:
                    load_mix(blk + 1)
                for tt in range(4):
                    c_tile(blk, tt)
                up(blk, 0)
                for fg in range(8):
                    ensure_w(blk * 8 + fg + 2)
                    if fg + 1 < 8:
                        up(blk, fg + 1)
                    down(blk, fg)
                for tt in range(4):
                    final_tile(blk, tt)
            pg.barrier()
            pg.emit()
    return nc


def _t5_bucket(dist):
    nb, md = 32, 128
    me = nb // 2
    d = np.maximum(dist, 1).astype(np.float32)
    large = me + (np.log(d / np.float32(me)) / np.float32(math.log(md / me)) * np.float32(nb - me))
    large = np.minimum(large.astype(np.int32), nb - 1)
    return np.where(dist < me, dist, large)


_NC_CACHE = {}


def kernel(x, ln_attn_pre, w_in, b_f, lam_q1, lam_k1, lam_q2, lam_k2, subln_g, rel_bias,
           w_out, ln_attn_post, ln_mlp_pre, w_up, w_down, ln_mlp_post):
    f32 = np.float32
    debug = bool(int(os.environ.get("KDEBUG", "0")))
    x = np.asarray(x, f32)
    rel_bias = np.asarray(rel_bias, f32)

    def trep(g):
        g = np.asarray(g, f32).reshape(8, 128)
        return np.ascontiguousarray(np.broadcast_to(g.T[:, :, None], (128, 8, 128))).reshape(128, 1024)

    def brep(g):
        return np.ascontiguousarray(np.broadcast_to(np.asarray(g, f32).reshape(1, 1024), (128, 1024)))

    kl = np.arange(128)[:, None]
    u = np.arange(640)[None, :]
    dd = u - kl
    bidx = _t5_bucket(np.maximum(dd, 0))
    tl = np.empty((128, 5, 640), f32)
    for h in range(4):
        tl[:, h, :] = np.where(dd >= 0, rel_bias[bidx, h], f32(NEGBIG))
    tl[:, 4, :] = np.where(dd >= 0, f32(0.0), f32(NEGBIG))
    cvec = np.ascontiguousarray(np.broadcast_to(rel_bias[31].reshape(1, 4), (128, 4)))
    lamv = np.concatenate([np.asarray(a, f32).reshape(-1) for a in (lam_q1, lam_k1, lam_q2, lam_k2)])
    lamv = np.ascontiguousarray(np.broadcast_to(lamv.reshape(1, 256), (128, 256)))

    common = {
        "w_in": np.ascontiguousarray(np.asarray(w_in, f32)[0]),
        "w_out": np.ascontiguousarray(np.asarray(w_out, f32)[0]),
        "w_up": np.ascontiguousarray(np.asarray(w_up, f32)[0]),
        "w_down": np.ascontiguousarray(np.asarray(w_down, f32)[0]),
        "g_preT": trep(ln_attn_pre),
        "g_mlpT": trep(ln_mlp_pre),
        "g_post_b": brep(ln_attn_post),
        "g_mpost_b": brep(ln_mlp_post),
        "bf": np.asarray(b_f, f32).reshape(8, 1).copy(),
        "lamv": lamv,
        "subg": np.asarray(subln_g, f32).reshape(128, 1).copy(),
        "cvec": cvec,
        "tl": tl,
        "ident": np.eye(128, dtype=f32),
    }
    if debug not in _NC_CACHE:
        _NC_CACHE[debug] = build_program(debug)
    nc = _NC_CACHE[debug]
    n = 8
    in_maps = []
    for c in range(n):
        m = dict(common)
        m["x"] = np.ascontiguousarray(x[c * NSEQ:(c + 1) * NSEQ])
        in_maps.append(m)
    res = run_bass_kernel_spmd(nc, in_maps, core_ids=list(range(n)))
    if debug:
        kernel.last = res
    out = np.concatenate([np.asarray(r["out"]).reshape(NSEQ, S, D) for r in res.results], axis=0)
    return out.astype(f32, copy=False)
```

```python
import math
import os
from contextlib import ExitStack

import numpy as np
import concourse.bass as bass
import concourse.mybir as mybir
from concourse.bass_utils import run_bass_kernel_spmd

F32 = mybir.dt.float32
BF16 = mybir.dt.bfloat16
AF = mybir.ActivationFunctionType
ALU = mybir.AluOpType

ENGS = ("pe", "act", "dve", "pool", "sp")

S = 4096
D = 1024
NSEQ = 2
INC = 3080
DFF = 4096
EPS = 1e-6
NEGBIG = -1e30


class Op:
    __slots__ = ("eng", "fn", "reads", "writes", "dma", "track", "idx", "deps", "marked", "semval")

    def __init__(self, eng, fn, reads, writes, dma):
        self.eng = eng
        self.fn = fn
        self.reads = reads
        self.writes = writes
        self.dma = dma
        self.deps = []
        self.marked = False


class Prog:
    def __init__(self, nc, tag):
        self.nc = nc
        self.tag = tag
        self.ops = []
        self.state = {}
        self.track_ops = {}
        self.seen = {e: {} for e in ENGS}

    def _add(self, op):
        track = op.track
        lst = self.track_ops.setdefault(track, [])
        op.idx = len(lst)
        lst.append(op)
        deps = {}
        st = self.state
        tops = self.track_ops

        def need(t, i):
            if not isinstance(t, str) and t != track:
                tops[t][i].marked = True
            if deps.get(t, -1) < i:
                deps[t] = i

        for k in op.reads:
            s = st.get(k)
            if s is not None and s[0] is not None:
                need(*s[0])
        for k in op.writes:
            s = st.get(k)
            if s is not None:
                if s[0] is not None:
                    need(*s[0])
                for t, i in s[1].items():
                    need(t, i)
        me = (track, op.idx)
        for k in op.reads:
            s = st.get(k)
            if s is None:
                s = st[k] = [None, {}]
            s[1][track] = op.idx
        for k in op.writes:
            st[k] = [me, {}]
        seen = self.seen[op.eng]
        for t, i in deps.items():
            if t == track and (op.eng == "pe" or op.dma):
                continue
            if seen.get(t, -1) >= i:
                continue
            seen[t] = i
            op.deps.append((t, i))
            self.track_ops[t][i].marked = True
        self.ops.append(op)
        return op

    def op(self, eng, fn, reads=(), writes=()):
        o = Op(eng, fn, tuple(reads), tuple(writes), False)
        o.track = eng
        return self._add(o)

    def dma(self, queue, sem, fn, reads=(), writes=()):
        o = Op(queue, fn, tuple(reads), tuple(writes), True)
        o.track = ("dma", sem)
        return self._add(o)

    def barrier(self):
        for e in ENGS:
            deps = {}
            for k, s in self.state.items():
                if s[0] is not None:
                    t, i = s[0]
                    if not isinstance(t, str):
                        self.track_ops[t][i].marked = True
                    if deps.get(t, -1) < i:
                        deps[t] = i
                for t, i in s[1].items():
                    if not isinstance(t, str):
                        self.track_ops[t][i].marked = True
                    if deps.get(t, -1) < i:
                        deps[t] = i
            o = Op(e, None, (), (), False)
            o.track = e
            lst = self.track_ops.setdefault(e, [])
            o.idx = len(lst)
            lst.append(o)
            seen = self.seen[e]
            for t, i in deps.items():
                if t == e and e == "pe":
                    continue
                if seen.get(t, -1) >= i:
                    continue
                seen[t] = i
                o.deps.append((t, i))
                self.track_ops[t][i].marked = True
            self.ops.append(o)
        self.state = {}

    def emit(self):
        nc = self.nc
        for t, lst in self.track_ops.items():
            c = 0
            for o in lst:
                if o.marked:
                    c += 1
                o.semval = c
        with ExitStack() as es:
            sems = {}
            for t in self.track_ops:
                if any(o.marked for o in self.track_ops[t]):
                    nm = self.tag + "_" + (t if isinstance(t, str) else "d_" + str(t[1]))
                    sems[t] = es.enter_context(nc.semaphore(nm))
            block = es.enter_context(nc.Block())
            per_eng = {e: [o for o in self.ops if o.eng == e] for e in ENGS}
            track_ops = self.track_ops

            def run(engobj, ename):
                for o in per_eng[ename]:
                    for (t, i) in o.deps:
                        tgt = track_ops[t][i]
                        mult = 1 if isinstance(t, str) else 16
                        engobj.wait_ge(sems[t], tgt.semval * mult)
                    if o.fn is None:
                        continue
                    ins = o.fn(engobj)
                    if o.marked:
                        ins.then_inc(sems[o.track], 16 if o.dma else 1)

            @block.tensor
            def _(e):
                run(e, "pe")

            @block.scalar
            def _(e):
                run(e, "act")

            @block.vector
            def _(e):
                run(e, "dve")

            @block.gpsimd
            def _(e):
                run(e, "pool")

            @block.sync
            def _(e):
                run(e, "sp")


def build_program(debug=False):
    nc = bass.Bass("TRN2", target_bir_lowering=False)
    dk = "ExternalOutput" if debug else "Internal"

    def din(name, shape, dt=F32):
        return nc.dram_tensor(name, shape, dt, kind="ExternalInput").ap()

    x_d = din("x", [NSEQ, S, D])
    win_d = din("w_in", [D, INC])
    wout_d = din("w_out", [D, D])
    wup_d = din("w_up", [D, DFF])
    wdn_d = din("w_down", [DFF, D])
    gpreT_d = din("g_preT", [128, D])
    gmlpT_d = din("g_mlpT", [128, D])
    gpostb_d = din("g_post_b", [128, D])
    gmpostb_d = din("g_mpost_b", [128, D])
    bf_d = din("bf", [8, 1])
    lam_d = din("lamv", [128, 256])
    subg_d = din("subg", [128, 1])
    cvec_d = din("cvec", [128, 4])
    tl_d = din("tl", [128, 5, 640])
    ident_d = din("ident", [128, 128])
    out_d = nc.dram_tensor("out", [NSEQ, S, D], F32, kind="ExternalOutput").ap()

    qk_d = nc.dram_tensor("qk_s", [NSEQ, 2048, S], BF16, kind=dk).ap()
    aux_d = nc.dram_tensor("aux_s", [NSEQ, 8, S], BF16, kind=dk).ap()
    v_d = nc.dram_tensor("v_s", [NSEQ, S, 1024], BF16, kind=dk).ap()
    mix_d = nc.dram_tensor("mix_s", [NSEQ, 1024, S], BF16, kind=dk).ap()
    wupb_d = nc.dram_tensor("wup_bf", [D, DFF], BF16, kind="Internal").ap()
    wdnb_d = nc.dram_tensor("wdn_bf", [DFF, D], BF16, kind="Internal").ap()

    with ExitStack() as es0:
        def T0(name, shape, dt):
            return es0.enter_context(nc.sbuf_tensor(name, shape, dt))

        idt = T0("idt", [128, 128], F32)
        ones_bf = T0("ones_bf", [128, 128], BF16)
        ones_f = T0("ones_f", [128, 128], F32)
        neglam = T0("neglam", [128, 1], F32)
        g08 = T0("g08", [128, 1], F32)
        negD = T0("negD", [128, NSEQ * 256], F32)
        cvt = T0("cvt", [128, 4], F32)
        negbf = T0("negbf", [8, 1], F32)

        with ExitStack() as es:
            def T(name, shape, dt):
                return es.enter_context(nc.sbuf_tensor(name, shape, dt))
            lamt = T("lamt", [128, 256], F32)
            lprod = T("lprod", [128, 128], F32)
            lsum = T("lsum", [128, 2], F32)
            lexp = T("lexp", [128, 2], F32)
            subg = T("subg_t", [128, 1], F32)
            bft = T("bft", [8, 1], F32)
            pg = Prog(nc, "p0")
            pg.dma("sp", "a", lambda e: e.dma_start(out=idt[:], in_=ident_d), writes=["idt"])
            pg.dma("sp", "b", lambda e: e.dma_start(out=lamt[:], in_=lam_d), writes=["lamt"])
            pg.dma("sp", "c", lambda e: e.dma_start(out=subg[:], in_=subg_d), writes=["subg"])
            pg.dma("sp", "d", lambda e: e.dma_start(out=cvt[:], in_=cvec_d), writes=["cvt"])
            pg.dma("sp", "e", lambda e: e.dma_start(out=bft[:], in_=bf_d), writes=["bft"])
            pg.op("dve", lambda e: e.memset(ones_bf[:], 1.0), writes=["ones_bf"])
            pg.op("dve", lambda e: e.memset(ones_f[:], 1.0), writes=["ones_f"])
            pg.op("dve", lambda e: e.tensor_tensor(out=lprod[:, 0:64], in0=lamt[:, 0:64], in1=lamt[:, 64:128], op=ALU.mult),
                  reads=["lamt"], writes=["lp0"])
            pg.op("dve", lambda e: e.tensor_tensor(out=lprod[:, 64:128], in0=lamt[:, 128:192], in1=lamt[:, 192:256], op=ALU.mult),
                  reads=["lamt"], writes=["lp1"])
            pg.op("dve", lambda e: e.reduce_sum(out=lsum[:, 0:1], in_=lprod[:, 0:64], axis=mybir.AxisListType.X),
                  reads=["lp0"], writes=["ls0"])
            pg.op("dve", lambda e: e.reduce_sum(out=lsum[:, 1:2], in_=lprod[:, 64:128], axis=mybir.AxisListType.X),
                  reads=["lp1"], writes=["ls1"])
            pg.op("act", lambda e: e.activation(out=lexp[:], in_=lsum[:], func=AF.Exp), reads=["ls0", "ls1"], writes=["lexp"])
            pg.op("dve", lambda e: e.tensor_tensor(out=neglam[:], in0=lexp[:, 1:2], in1=lexp[:, 0:1], op=ALU.subtract),
                  reads=["lexp"], writes=["neglam"])
            pg.op("dve", lambda e: e.tensor_scalar(out=neglam[:], in0=neglam[:], scalar1=-0.2, scalar2=None, op0=ALU.add),
                  reads=["neglam"], writes=["neglam"])
            pg.op("dve", lambda e: e.tensor_scalar(out=g08[:], in0=subg[:], scalar1=0.8, scalar2=None, op0=ALU.mult),
                  reads=["subg"], writes=["g08"])
            pg.op("dve", lambda e: e.tensor_scalar(out=negbf[:], in0=bft[:], scalar1=-1.0, scalar2=None, op0=ALU.mult),
                  reads=["bft"], writes=["negbf"])
            pg.barrier()
            pg.emit()

        with ExitStack() as es:
            def T(name, shape, dt):
                return es.enter_context(nc.sbuf_tensor(name, shape, dt))

            def PS(name, shape, dt=F32):
                return es.enter_context(nc.psum_tensor(name, shape, dt))
            winb = T("winb", [128, 8, INC], BF16)
            gT = T("gT", [128, 8, 128], F32)
            xt = [T(f"xt{i}", [128, D], F32) for i in range(2)]
            xs = [T(f"xs{i}", [128, D], F32) for i in range(2)]
            sqj = T("sqj", [128, D], BF16)
            ss = [T(f"ss{i}", [128, 1], F32) for i in range(2)]
            lnv = [T(f"lnv{i}", [128, 1], F32) for i in range(2)]
            rs = [T(f"rs{i}", [128, 1], F32) for i in range(2)]
            xnT = [T(f"xnT{i}", [128, 8, 512], BF16) for i in range(2)]
            stQK = [T(f"stQK{i}", [128, 16, 512], BF16) for i in range(2)]
            stV = [T(f"stV{i}", [128, 4, 1024], BF16) for i in range(2)]
            e8 = T("e8", [8, 512], F32)
            sp8 = T("sp8", [8, 512], F32)
            ones8 = T("ones8", [8, 512], F32)
            cumT = T("cumT", [8, S], F32)
            cumb = T("cumb", [8, S], BF16)
            tp = [[PS(f"tp{i}{h}", [128, 512]) for h in range(2)] for i in range(2)]
            mm = [PS(f"mm{i}", [128, 512]) for i in range(3)]

            pg = Prog(nc, "pA")
            win_v = win_d.rearrange("(kc p) c -> p kc c", p=128)
            pg.dma("pool", "win0", lambda e: e.dma_start(out=winb[:, :, 0:1540], in_=win_v[:, :, 0:1540]), writes=["winb0"])
            pg.dma("pool", "win1", lambda e: e.dma_start(out=winb[:, :, 1540:INC], in_=win_v[:, :, 1540:INC]), writes=["winb1"])
            pg.dma("sp", "gT", lambda e: e.dma_start(out=gT[:], in_=gpreT_d.rearrange("p (k j) -> p k j", j=128)), writes=["gT"])
            pg.op("dve", lambda e: e.memset(ones8[:], 1.0), writes=["ones8"])
            wup_v = wup_d.rearrange("r (a c) -> (r a) c", c=2048)
            wupb_v = wupb_d.rearrange("r (a c) -> (r a) c", c=2048)
            for q in range(4):
                pg.dma("pool", f"wupc{q}", lambda e, q=q: e.dma_start(out=wupb_v[q * 512:(q + 1) * 512, :], in_=wup_v[q * 512:(q + 1) * 512, :]),
                       writes=[("wupb", q)])
            wdn_v = wdn_d.rearrange("(r a) c -> r (a c)", a=2)
            wdnb_v = wdnb_d.rearrange("(r a) c -> r (a c)", a=2)
            for q in range(4):
                pg.dma("pool", f"wdnc{q}", lambda e, q=q: e.dma_start(out=wdnb_v[q * 512:(q + 1) * 512, :], in_=wdn_v[q * 512:(q + 1) * 512, :]),
                       writes=[("wdnb", q)])

            def qk_cols(oc):
                if oc < 4:
                    return oc * 128, 0.125
                if oc < 8:
                    return 512 + (oc - 4) * 128, 1.0
                if oc < 12:
                    return 1536 + (oc - 8) * 128, 0.125
                return 2048 + (oc - 12) * 128, 1.0

            mmc = [0]

            def next_mm():
                b = mmc[0] % 3
                mmc[0] += 1
                return b

            evc = [0]
            for s in range(NSEQ):
                for tb in range(8):
                    blk = s * 8 + tb
                    xn = xnT[blk % 2]
                    xnk = ("xnT", blk % 2)
                    for tt in range(4):
                        ti = blk * 4 + tt
                        sl = ti % 2
                        r0 = tb * 512 + tt * 128
                        pg.dma("sp", f"x{sl}", lambda e, sl=sl, s=s, r0=r0: e.dma_start(out=xt[sl][:], in_=x_d[s, r0:r0 + 128, :]),
                               writes=[("xt", sl)])
                        pg.op("act", lambda e, sl=sl: e.activation(out=sqj[:], in_=xt[sl][:], func=AF.Square, accum_out=ss[sl][:]),
                              reads=[("xt", sl)], writes=[("ss", sl)])
                        pg.op("act", lambda e, sl=sl: e.activation(out=lnv[sl][:], in_=ss[sl][:], func=AF.Ln, scale=1.0 / D, bias=EPS),
                              reads=[("ss", sl)], writes=[("lnv", sl)])
                        pg.op("act", lambda e, sl=sl: e.activation(out=rs[sl][:], in_=lnv[sl][:], func=AF.Exp, scale=-0.5),
                              reads=[("lnv", sl)], writes=[("rs", sl)])
                        pg.op("act", lambda e, sl=sl: e.activation(out=xs[sl][:], in_=xt[sl][:], func=AF.Copy, scale=rs[sl][:]),
                              reads=[("xt", sl), ("rs", sl)], writes=[("xs", sl)])
                        for half in range(2):
                            for j in range(4):
                                kc = half * 4 + j
                                pg.op("pe", lambda e, sl=sl, half=half, j=j, kc=kc: e.transpose(
                                    out=tp[sl][half][:, j * 128:(j + 1) * 128], in_=xs[sl][:, kc * 128:(kc + 1) * 128], identity=idt[:]),
                                    reads=[("xs", sl), "idt"], writes=[("tp", sl, half)])
                            pg.op("dve", lambda e, sl=sl, half=half, xn=xn, tt=tt: e.tensor_tensor(
                                out=xn[:, half * 4:(half + 1) * 4, tt * 128:(tt + 1) * 128],
                                in0=tp[sl][half][:].rearrange("p (k j) -> p k j", j=128),
                                in1=gT[:, half * 4:(half + 1) * 4, :], op=ALU.mult),
                                reads=[("tp", sl, half), "gT"], writes=[xnk])
                    stq = stQK[blk % 2]
                    for oc in range(16):
                        c0, scl = qk_cols(oc)
                        b = next_mm()
                        for kc in range(8):
                            pg.op("pe", lambda e, b=b, kc=kc, c0=c0, xn=xn: e.matmul(
                                mm[b][:], lhsT=winb[:, kc, c0:c0 + 128], rhs=xn[:, kc, :], start=(kc == 0), stop=(kc == 7)),
                                reads=[xnk, "winb0", "winb1"], writes=[("mm", b)])
                        evc[0] += 1
                        if evc[0] % 2 == 0:
                            pg.op("act", lambda e, b=b, oc=oc, scl=scl, stq=stq: e.activation(
                                out=stq[:, oc, :], in_=mm[b][:], func=AF.Copy, scale=scl),
                                reads=[("mm", b)], writes=[("stQK", blk % 2)])
                        else:
                            pg.op("dve", lambda e, b=b, oc=oc, scl=scl, stq=stq: e.tensor_scalar(
                                out=stq[:, oc, :], in0=mm[b][:], scalar1=scl, scalar2=None, op0=ALU.mult),
                                reads=[("mm", b)], writes=[("stQK", blk % 2)])
                    pg.dma("pool", f"sqk{blk % 2}", lambda e, s=s, tb=tb, stq=stq: e.dma_start(
                        out=qk_d[s].rearrange("(oc p) t -> p oc t", p=128)[:, :, tb * 512:(tb + 1) * 512], in_=stq[:]),
                        reads=[("stQK", blk % 2)], writes=[("qk_d", s, tb)])
                    stv = stV[blk % 2]
                    for tt in range(4):
                        for half in range(2):
                            vc0 = 1024 if half == 0 else 2560
                            b = next_mm()
                            for kc in range(8):
                                pg.op("pe", lambda e, b=b, kc=kc, vc0=vc0, xn=xn, tt=tt: e.matmul(
                                    mm[b][:], lhsT=xn[:, kc, tt * 128:(tt + 1) * 128], rhs=winb[:, kc, vc0:vc0 + 512],
                                    start=(kc == 0), stop=(kc == 7)),
                                    reads=[xnk, "winb0", "winb1"], writes=[("mm", b)])
                            evc[0] += 1
                            if evc[0] % 2 == 0:
                                pg.op("act", lambda e, b=b, tt=tt, half=half, stv=stv: e.activation(
                                    out=stv[:, tt, half * 512:(half + 1) * 512], in_=mm[b][:], func=AF.Copy),
                                    reads=[("mm", b)], writes=[("stV", blk % 2)])
                            else:
                                pg.op("dve", lambda e, b=b, tt=tt, half=half, stv=stv: e.tensor_copy(
                                    out=stv[:, tt, half * 512:(half + 1) * 512], in_=mm[b][:]),
                                    reads=[("mm", b)], writes=[("stV", blk % 2)])
                    pg.dma("pool", f"sv{blk % 2}", lambda e, s=s, tb=tb, stv=stv: e.dma_start(
                        out=v_d[s, tb * 512:(tb + 1) * 512, :].rearrange("(tt p) c -> p tt c", p=128), in_=stv[:]),
                        reads=[("stV", blk % 2)], writes=[("v_d", s, tb)])
                    b = next_mm()
                    for kc in range(8):
                        pg.op("pe", lambda e, b=b, kc=kc, xn=xn: e.matmul(
                            mm[b][0:8, :], lhsT=winb[:, kc, 3072:3080], rhs=xn[:, kc, :], start=(kc == 0), stop=(kc == 7)),
                            reads=[xnk, "winb0", "winb1"], writes=[("mm", b)])
                    pg.op("act", lambda e, b=b: e.activation(out=e8[:], in_=mm[b][0:8, :], func=AF.Exp, scale=-1.0, bias=negbf[:]),
                          reads=[("mm", b), "negbf"], writes=["e8"])
                    pg.op("act", lambda e: e.activation(out=sp8[:], in_=e8[:], func=AF.Ln, bias=1.0),
                          reads=["e8"], writes=["sp8"])
                    if tb == 0:
                        pg.op("dve", lambda e: e.tensor_tensor_scan(out=cumT[:, 0:512], data0=ones8[:], data1=sp8[:], initial=0.0,
                                                                    op0=ALU.mult, op1=ALU.subtract),
                              reads=["sp8", "ones8"], writes=["cumT"])
                    else:
                        pg.op("dve", lambda e, tb=tb: e.tensor_tensor_scan(
                            out=cumT[:, tb * 512:(tb + 1) * 512], data0=ones8[:], data1=sp8[:],
                            initial=cumT[:, tb * 512 - 1:tb * 512], op0=ALU.mult, op1=ALU.subtract),
                            reads=["sp8", "ones8", "cumT"], writes=["cumT"])
                pg.op("dve", lambda e: e.tensor_copy(out=cumb[:], in_=cumT[:]), reads=["cumT"], writes=["cumb"])
                pg.dma("sp", "aux", lambda e, s=s: e.dma_start(out=aux_d[s], in_=cumb[:]), reads=["cumb"], writes=[("aux_d", s)])
                b = next_mm()
                for kb in range(32):
                    pg.op("pe", lambda e, b=b, kb=kb: e.transpose(
                        out=mm[b][:, kb * 8:(kb + 1) * 8], in_=cumT[0:8, kb * 128:(kb + 1) * 128], identity=idt[0:8, 0:8]),
                        reads=["cumT", "idt"], writes=[("mm", b)])
                pg.op("dve", lambda e, b=b, s=s: e.tensor_scalar(
                    out=negD[:, s * 256:(s + 1) * 256], in0=mm[b][:, 0:256], scalar1=-1.0, scalar2=None, op0=ALU.mult),
                    reads=[("mm", b)], writes=["negD"])
            pg.barrier()
            pg.emit()

        with ExitStack() as es:
            def T(name, shape, dt):
                return es.enter_context(nc.sbuf_tensor(name, shape, dt))

            def PS(name, shape, dt=F32):
                return es.enter_context(nc.psum_tensor(name, shape, dt))
            QT = [T(f"QT{i}", [128, S], BF16) for i in range(2)]
            QT2 = [T(f"QT2{i}", [128, S], BF16) for i in range(2)]
            KT = [T(f"KT{i}", [128, S], BF16) for i in range(2)]
            VT = [T(f"VT{i}", [128, 32, 128], BF16) for i in range(2)]
            TL = T("TL", [128, 5, 640], F32)
            NPT = 6
            NTB = 4
            PT = [T(f"PT{i}", [128, 512], BF16) for i in range(NPT)]
            tmpb = [T(f"tmpb{i}", [128, 512], F32) for i in range(NTB)]
            mix = [T(f"mix{i}", [128, S], BF16) for i in range(2)]
            R0 = T("R0", [128, 512], F32)
            R1 = T("R1", [128, 512], F32)
            Tt0 = T("Tt0", [128, 512], F32)
            Tt1 = T("Tt1", [128, 512], F32)
            od = T("od", [128, 512], F32)
            sqd = T("sqd", [128, 512], F32)
            lnd = T("lnd", [128, 512], F32)
            rsd = T("rsd", [128, 512], F32)
            NSB = 4
            Sb = [PS(f"Sb{i}", [128, 512]) for i in range(NSB)]
            Ob = [PS(f"Ob{i}", [128, 512]) for i in range(2)]
            Lb = [PS(f"Lb{i}", [128, 512]) for i in range(2)]

            pg = Prog(nc, "pB")
            pg.dma("sp", "tl", lambda e: e.dma_start(out=TL[:], in_=tl_d), writes=["TL"])

            units = []
            for s in range(NSEQ):
                for h in range(4):
                    units.append((s, "d", h))
                for h in range(8):
                    units.append((s, "f", h))

            def load_unit(ui):
                s, kind, h = units[ui]
                sl = ui % 2
                if kind == "d":
                    r = h * 128
                    pg.op("pool", lambda e: e.memset(QT[sl][64:128, :], 0.0), writes=[("Qhi", sl)])
                    pg.op("pool", lambda e: e.memset(QT2[sl][0:64, :], 0.0), writes=[("Q2lo", sl)])
                    pg.dma("sp", f"q{sl}", lambda e: e.dma_start(out=QT[sl][0:64, :], in_=qk_d[s, r:r + 64, :]),
                           writes=[("Qlo", sl)])
                    pg.dma("sp", f"q{sl}", lambda e: e.dma_start(out=QT2[sl][64:128, :], in_=qk_d[s, r + 64:r + 128, :]),
                           writes=[("Q2hi", sl)])
                    pg.dma("sp", f"k{sl}", lambda e: e.dma_start(out=KT[sl][:], in_=qk_d[s, (4 + h) * 128:(5 + h) * 128, :]),
                           writes=[("Klo", sl), ("Khi", sl)])
                    pg.dma("sp", f"v{sl}", lambda e: e.dma_start(
                        out=VT[sl][:], in_=v_d[s, :, h * 128:(h + 1) * 128].rearrange("(kb p) c -> p kb c", p=128)),
                        writes=[("Vlo", sl), ("Vhi", sl)])
                else:
                    qr = (8 + h // 2) * 128 + (h % 2) * 64
                    kr = (12 + h // 2) * 128 + (h % 2) * 64
                    po = (h % 2) * 64
                    oth = 64 - po
                    vk = ("Vlo", sl) if po == 0 else ("Vhi", sl)
                    vok = ("Vhi", sl) if po == 0 else ("Vlo", sl)
                    pg.op("pool", lambda e: e.memset(KT[sl][64:65, :], 1.0), writes=[("Khi", sl)])
                    pg.op("pool", lambda e: e.memset(VT[sl][:, :, oth:oth + 64], 0.0), writes=[vok])
                    pg.dma("sp", f"q{sl}", lambda e: e.dma_start(out=QT[sl][0:64, :], in_=qk_d[s, qr:qr + 64, :]),
                           writes=[("Qlo", sl)])
                    pg.dma("sp", f"q{sl}", lambda e: e.dma_start(out=QT[sl][64:65, :], in_=aux_d[s, h:h + 1, :]),
                           writes=[("Qhi", sl)])
                    pg.dma("sp", f"k{sl}", lambda e: e.dma_start(out=KT[sl][0:64, :], in_=qk_d[s, kr:kr + 64, :]),
                           writes=[("Klo", sl)])
                    pg.dma("sp", f"v{sl}", lambda e: e.dma_start(
                        out=VT[sl][:, :, po:po + 64], in_=v_d[s, :, 512 + h * 64:512 + (h + 1) * 64].rearrange("(kb p) c -> p kb c", p=128)),
                        writes=[vk])

            sctr = [0]
            pctr = [0]
            tctr = [0]
            G = 2

            def run_unit(ui):
                s, kind, h = units[ui]
                sl = ui % 2
                nstream = 2 if kind == "d" else 1
                if kind == "d":
                    chunk = h
                    po = 0
                    tli = h
                else:
                    chunk = 4 + h // 2
                    po = (h % 2) * 64
                    tli = 4
                csl = chunk % 2
                kkeys = [("Klo", sl), ("Khi", sl)]
                qkeys = [[("Qlo", sl), ("Qhi", sl)], [("Q2lo", sl), ("Q2hi", sl)]]
                vkeys = [("Vlo", sl), ("Vhi", sl)]
                blocks = []
                for qb in range(8):
                    nkb = 4 * qb + 4
                    for kb in range(nkb):
                        for i in range(nstream):
                            blocks.append((qb, kb, i, kb == nkb - 1 and i == nstream - 1))
                info = {}

                def emit_qk(n):
                    qb, kb, i, _ = blocks[n]
                    off = qb * 512 - kb * 128
                    c0 = max(0, -off)
                    sb = sctr[0] % NSB
                    sctr[0] += 1
                    ps = pctr[0] % NPT
                    pctr[0] += 1
                    info[n] = (sb, c0, off, ps)
                    if kind == "d":
                        pr = slice(0, 128)
                        qt = QT[sl] if i == 0 else QT2[sl]
                    else:
                        pr = slice(0, 65)
                        qt = QT[sl]
                    pg.op("pe", lambda e: e.matmul(
                        Sb[sb][:, c0:512], lhsT=KT[sl][pr, kb * 128:(kb + 1) * 128],
                        rhs=qt[pr, qb * 512 + c0:(qb + 1) * 512], start=True, stop=True),
                        reads=qkeys[i] + kkeys, writes=[("S", sb)])

                def emit_act(n):
                    qb, kb, i, lastq = blocks[n]
                    sb, c0, off, ps = info[n]
                    special = (off <= 128) if kind == "d" else (off <= 0)
                    if kind == "d":
                        bias_ap = None if special else cvt[:, h:h + 1]
                    else:
                        ix = (s * 32 + kb) * 8 + h
                        bias_ap = negD[:, ix:ix + 1]
                    if special:
                        ts = tctr[0] % NTB
                        tctr[0] += 1
                        t0 = max(off, 0)
                        pg.op("dve", lambda e: e.tensor_tensor(
                            out=tmpb[ts][:, c0:512], in0=Sb[sb][:, c0:512], in1=TL[:, tli, t0:t0 + 512 - c0], op=ALU.add),
                            reads=[("S", sb), "TL"], writes=[("tmp", ts)])
                        src = tmpb[ts]
                        rk = ("tmp", ts)
                    else:
                        src = Sb[sb]
                        rk = ("S", sb)
                    if bias_ap is None:
                        pg.op("act", lambda e: e.activation(out=PT[ps][:, c0:512], in_=src[:, c0:512], func=AF.Exp),
                              reads=[rk], writes=[("PT", ps)])
                    else:
                        pg.op("act", lambda e: e.activation(out=PT[ps][:, c0:512], in_=src[:, c0:512], func=AF.Exp, bias=bias_ap),
                              reads=[rk, "negD", "cvt"], writes=[("PT", ps)])

                def emit_pv(n):
                    qb, kb, i, lastq = blocks[n]
                    sb, c0, off, ps = info.pop(n)
                    nkb = 4 * qb + 4
                    pg.op("pe", lambda e: e.matmul(
                        Ob[i][:, c0:512], lhsT=VT[sl][:, kb, :], rhs=PT[ps][:, c0:512],
                        start=(kb == 0), stop=(kb == nkb - 1)),
                        reads=vkeys + [("PT", ps)], writes=[("O", i)])
                    pg.op("pe", lambda e: e.matmul(
                        Lb[i][:, c0:512], lhsT=ones_bf[:, :], rhs=PT[ps][:, c0:512],
                        start=(kb == 0), stop=(kb == nkb - 1)),
                        reads=["ones_bf", ("PT", ps)], writes=[("L", i)])
                    if lastq:
                        finalize(qb)

                def finalize(qb):
                    cs = slice(qb * 512, (qb + 1) * 512)
                    mk = ("mix", csl)
                    if kind == "d":
                        pg.op("dve", lambda e: e.reciprocal(out=R0[:], in_=Lb[0][:]), reads=[("L", 0)], writes=["R0"])
                        pg.op("dve", lambda e: e.tensor_tensor(out=Tt0[:], in0=Ob[0][:], in1=R0[:], op=ALU.mult),
                              reads=[("O", 0), "R0"], writes=["T0"])
                        pg.op("dve", lambda e: e.reciprocal(out=R1[:], in_=Lb[1][:]), reads=[("L", 1)], writes=["R1"])
                        pg.op("dve", lambda e: e.tensor_tensor(out=Tt1[:], in0=Ob[1][:], in1=R1[:], op=ALU.mult),
                              reads=[("O", 1), "R1"], writes=["T1"])
                        pg.op("dve", lambda e: e.scalar_tensor_tensor(
                            out=od[:], in0=Tt1[:], scalar=neglam[:, 0:1], in1=Tt0[:], op0=ALU.mult, op1=ALU.add),
                            reads=["T0", "T1"], writes=["od"])
                        pg.op("pool", lambda e: e.tensor_tensor(out=sqd[:], in0=od[:], in1=od[:], op=ALU.mult),
                              reads=["od"], writes=["sqd"])
                        pg.op("pe", lambda e: e.matmul(Lb[1][:], lhsT=ones_f[:], rhs=sqd[:], start=True, stop=True),
                              reads=["sqd", "ones_f"], writes=[("L", 1)])
                        pg.op("act", lambda e: e.activation(out=lnd[:], in_=Lb[1][:], func=AF.Ln, scale=1.0 / 128, bias=EPS),
                              reads=[("L", 1)], writes=["lnd"])
                        pg.op("act", lambda e: e.activation(out=rsd[:], in_=lnd[:], func=AF.Exp, scale=-0.5),
                              reads=["lnd"], writes=["rsd"])
                        pg.op("dve", lambda e: e.scalar_tensor_tensor(
                            out=mix[csl][:, cs], in0=od[:], scalar=g08[:, 0:1], in1=rsd[:], op0=ALU.mult, op1=ALU.mult),
                            reads=["od", "rsd"], writes=[mk])
                    else:
                        pp = slice(po, po + 64)
                        pg.op("dve", lambda e: e.reciprocal(out=R0[pp, :], in_=Lb[0][pp, :]), reads=[("L", 0)], writes=["R0"])
                        pg.op("dve", lambda e: e.tensor_tensor(out=mix[csl][pp, cs], in0=Ob[0][pp, :], in1=R0[pp, :], op=ALU.mult),
                              reads=[("O", 0), "R0"], writes=[mk])

                nb = len(blocks)
                batches = [list(range(j, min(j + G, nb))) for j in range(0, nb, G)]
                for n in batches[0]:
                    emit_qk(n)
                for bi, bt in enumerate(batches):
                    if bi + 1 < len(batches):
                        for n in batches[bi + 1]:
                            emit_qk(n)
                    for n in bt:
                        emit_act(n)
                    for n in bt:
                        emit_pv(n)
                if kind == "d" or (h % 2 == 1):
                    pg.dma("sp", f"mx{csl}", lambda e: e.dma_start(out=mix_d[s, chunk * 128:(chunk + 1) * 128, :], in_=mix[csl][:]),
                           reads=[("mix", csl)], writes=[("mix_d", s, chunk)])

            load_unit(0)
            for ui in range(len(units)):
                if ui + 1 < len(units):
                    load_unit(ui + 1)
                run_unit(ui)
            pg.barrier()
            pg.emit()

        with ExitStack() as es:
            def T(name, shape, dt):
                return es.enter_context(nc.sbuf_tensor(name, shape, dt))

            def PS(name, shape, dt=F32):
                return es.enter_context(nc.psum_tensor(name, shape, dt))
            woutb = T("woutb", [128, 8, D], BF16)
            gpostb = T("gpostb", [128, D], F32)
            gmpostb = T("gmpostb", [128, D], F32)
            g2T = T("g2T", [128, 8, 128], F32)
            mixb = [T(f"mixb{i}", [128, 8, 512], BF16) for i in range(2)]
            xt = [T(f"cxt{i}", [128, D], F32) for i in range(2)]
            x1 = T("x1", [128, 4, D], F32)
            xs2 = [T(f"xs2{i}", [128, D], F32) for i in range(2)]
            sqj = T("csqj", [128, D], BF16)
            h2T = [T(f"h2T{i}", [128, 8, 512], BF16) for i in range(2)]
            wup = [T(f"wup{i}", [128, 8, 512], BF16) for i in range(3)]
            wdn = [T(f"wdn{i}", [128, 4, D], BF16) for i in range(3)]
            rl = [T(f"rl{i}", [128, 512], F32) for i in range(2)]
            uT = [T(f"uT{i}", [128, 4, 512], BF16) for i in range(2)]
            acc = T("acc", [128, 4, D], F32)
            tmpn = T("tmpn", [128, D], F32)
            outst = [T(f"outst{i}", [128, D], F32) for i in range(2)]
            ssc = [T(f"ssc{i}", [128, 1], F32) for i in range(6)]
            lnc = [T(f"lnc{i}", [128, 1], F32) for i in range(6)]
            rsc = [T(f"rsc{i}", [128, 1], F32) for i in range(6)]
            Y = PS("Y", [128, 1024])
            TP = [PS(f"TP{i}", [128, 512]) for i in range(2)]
            U = [PS(f"U{i}", [128, 512]) for i in range(2)]
            Yd = [PS(f"Yd{i}", [128, 512]) for i in range(2)]

            pg = Prog(nc, "pC")
            pg.dma("pool", "wo", lambda e: e.dma_start(out=woutb[:], in_=wout_d.rearrange("(kc p) c -> p kc c", p=128)), writes=["woutb"])
            pg.dma("sp", "g1", lambda e: e.dma_start(out=gpostb[:], in_=gpostb_d), writes=["gpostb"])
            pg.dma("sp", "g2", lambda e: e.dma_start(out=gmpostb[:], in_=gmpostb_d), writes=["gmpostb"])
            pg.dma("sp", "g3", lambda e: e.dma_start(out=g2T[:], in_=gmlpT_d.rearrange("p (k j) -> p k j", j=128)), writes=["g2T"])

            nblk = NSEQ * 8
            nrm = [0]

            def rstd_from(src_ap, srckeys, n_el):
                k = nrm[0] % 6
                nrm[0] += 1
                pg.op("act", lambda e: e.activation(out=sqj[:], in_=src_ap, func=AF.Square, accum_out=ssc[k][:]),
                      reads=srckeys, writes=[("ssc", k)])
                pg.op("act", lambda e: e.activation(out=lnc[k][:], in_=ssc[k][:], func=AF.Ln, scale=1.0 / n_el, bias=EPS),
                      reads=[("ssc", k)], writes=[("lnc", k)])
                pg.op("act", lambda e: e.activation(out=rsc[k][:], in_=lnc[k][:], func=AF.Exp, scale=-0.5),
                      reads=[("lnc", k)], writes=[("rsc", k)])
                return rsc[k], ("rsc", k)

            NW = 3
            witems = [(blk, fg) for blk in range(nblk) for fg in range(8)]
            wnext = [0]

            def ensure_w(upto):
                while wnext[0] <= upto and wnext[0] < len(witems):
                    load_w(wnext[0])
                    wnext[0] += 1

            def load_w(item):
                blk, fg = witems[item]
                i = item % NW
                pg.dma("sp", f"wu{i}", lambda e: e.dma_start(
                    out=wup[i][:], in_=wupb_d.rearrange("(kc p) c -> p kc c", p=128)[:, :, fg * 512:(fg + 1) * 512]),
                    reads=[("wupb", q) for q in range(4)], writes=[("wup", i)])
                pg.dma("sp", f"wd{i}", lambda e: e.dma_start(
                    out=wdn[i][:], in_=wdnb_d[fg * 512:(fg + 1) * 512, :].rearrange("(fc p) c -> p fc c", p=128)),
                    reads=[("wdnb", q) for q in range(4)], writes=[("wdn", i)])

            def load_mix(blk):
                s, tb = divmod(blk, 8)
                i = blk % 2
                pg.dma("sp", f"mb{i}", lambda e: e.dma_start(
                    out=mixb[i][:], in_=mix_d[s].rearrange("(kc p) t -> p kc t", p=128)[:, :, tb * 512:(tb + 1) * 512]),
                    writes=[("mixb", i)])

            xl = [0]
            ost = [0]

            def c_tile(blk, tt):
                s, tb = divmod(blk, 8)
                mb = mixb[blk % 2]
                hT = h2T[blk % 2]
                hk = ("h2T", blk % 2)
                r0 = tb * 512 + tt * 128
                xi = xl[0] % 2
                xl[0] += 1
                pg.dma("sp", f"cx{xi}", lambda e: e.dma_start(out=xt[xi][:], in_=x_d[s, r0:r0 + 128, :]),
                       writes=[("xt", xi)])
                for half in range(2):
                    for kc in range(8):
                        pg.op("pe", lambda e, half=half, kc=kc: e.matmul(
                            Y[:, half * 512:(half + 1) * 512], lhsT=mb[:, kc, tt * 128:(tt + 1) * 128],
                            rhs=woutb[:, kc, half * 512:(half + 1) * 512], start=(kc == 0), stop=(kc == 7)),
                            reads=[("mixb", blk % 2), "woutb"], writes=["Y"])
                rt, rk = rstd_from(Y[:], ["Y"], D)
                pg.op("dve", lambda e: e.scalar_tensor_tensor(
                    out=tmpn[:], in0=Y[:], scalar=rt[:, 0:1], in1=gpostb[:], op0=ALU.mult, op1=ALU.mult),
                    reads=["Y", rk, "gpostb"], writes=["tmpn"])
                pg.op("dve", lambda e: e.tensor_tensor(out=x1[:, tt, :], in0=tmpn[:], in1=xt[xi][:], op=ALU.add),
                      reads=["tmpn", ("xt", xi)], writes=[("x1", tt)])
                rt2, rk2 = rstd_from(x1[:, tt, :], [("x1", tt)], D)
                xsi = xi
                pg.op("act", lambda e: e.activation(out=xs2[xsi][:], in_=x1[:, tt, :], func=AF.Copy, scale=rt2[:]),
                      reads=[("x1", tt), rk2], writes=[("xs2", xsi)])
                for half in range(2):
                    for j in range(4):
                        kc = half * 4 + j
                        pg.op("pe", lambda e, half=half, j=j, kc=kc: e.transpose(
                            out=TP[half][:, j * 128:(j + 1) * 128], in_=xs2[xsi][:, kc * 128:(kc + 1) * 128], identity=idt[:]),
                            reads=[("xs2", xsi), "idt"], writes=[("TP", half)])
                    pg.op("dve", lambda e, half=half: e.tensor_tensor(
                        out=hT[:, half * 4:(half + 1) * 4, tt * 128:(tt + 1) * 128],
                        in0=TP[half][:].rearrange("p (k j) -> p k j", j=128),
                        in1=g2T[:, half * 4:(half + 1) * 4, :], op=ALU.mult),
                        reads=[("TP", half), "g2T"], writes=[hk])

            def up(blk, fg):
                hT = h2T[blk % 2]
                hk = ("h2T", blk % 2)
                wi = (blk * 8 + fg) % NW
                ui_ = fg % 2
                for fcl in range(4):
                    ub = (fg * 4 + fcl) % 2
                    for kc in range(8):
                        pg.op("pe", lambda e, ub=ub, kc=kc, fcl=fcl: e.matmul(
                            U[ub][:], lhsT=wup[wi][:, kc, fcl * 128:(fcl + 1) * 128], rhs=hT[:, kc, :],
                            start=(kc == 0), stop=(kc == 7)),
                            reads=[("wup", wi), hk], writes=[("U", ub)])
                    pg.op("act", lambda e, ub=ub: e.activation(out=rl[ub][:], in_=U[ub][:], func=AF.Relu),
                          reads=[("U", ub)], writes=[("rl", ub)])
                    pg.op("pool", lambda e, ub=ub, fcl=fcl: e.tensor_tensor(
                        out=uT[ui_][:, fcl, :], in0=rl[ub][:], in1=rl[ub][:], op=ALU.mult),
                        reads=[("rl", ub)], writes=[("uT", ui_)])

            def down(blk, fg):
                wi = (blk * 8 + fg) % NW
                ui_ = fg % 2
                for tt in range(4):
                    for half in range(2):
                        yb = half
                        for fcl in range(4):
                            pg.op("pe", lambda e, yb=yb, fcl=fcl, tt=tt, half=half: e.matmul(
                                Yd[yb][:], lhsT=uT[ui_][:, fcl, tt * 128:(tt + 1) * 128],
                                rhs=wdn[wi][:, fcl, half * 512:(half + 1) * 512], start=(fcl == 0), stop=(fcl == 3)),
                                reads=[("uT", ui_), ("wdn", wi)], writes=[("Yd", yb)])
                        ak = ("acc", tt, half)
                        if fg == 0:
                            pg.op("dve", lambda e, yb=yb, tt=tt, half=half: e.tensor_copy(
                                out=acc[:, tt, half * 512:(half + 1) * 512], in_=Yd[yb][:]),
                                reads=[("Yd", yb)], writes=[ak])
                        else:
                            pg.op("dve", lambda e, yb=yb, tt=tt, half=half: e.tensor_tensor(
                                out=acc[:, tt, half * 512:(half + 1) * 512], in0=Yd[yb][:],
                                in1=acc[:, tt, half * 512:(half + 1) * 512], op=ALU.add),
                                reads=[("Yd", yb), ak], writes=[ak])

            def final_tile(blk, tt):
                s, tb = divmod(blk, 8)
                r0 = tb * 512 + tt * 128
                rt3, rk3 = rstd_from(acc[:, tt, :], [("acc", tt, 0), ("acc", tt, 1)], D)
                oi = ost[0] % 2
                ost[0] += 1
                pg.op("dve", lambda e: e.scalar_tensor_tensor(
                    out=outst[oi][:], in0=acc[:, tt, :], scalar=rt3[:, 0:1], in1=gmpostb[:], op0=ALU.mult, op1=ALU.mult),
                    reads=[("acc", tt, 0), ("acc", tt, 1), rk3, "gmpostb"], writes=[("ost", oi)])
                pg.op("pool", lambda e: e.tensor_tensor(out=outst[oi][:], in0=outst[oi][:], in1=x1[:, tt, :], op=ALU.add),
                      reads=[("ost", oi), ("x1", tt)], writes=[("ost", oi)])
                pg.dma("pool", f"o{oi}", lambda e: e.dma_start(out=out_d[s, r0:r0 + 128, :], in_=outst[oi][:]),
                       reads=[("ost", oi)], writes=[("out", blk, tt)])

            load_mix(0)
            ensure_w(1)
            for blk in range(nblk):
                if blk + 1 < nblk:
                    load_mix(blk + 1)
                for tt in range(4):
                    c_tile(blk, tt)
                up(blk, 0)
                for fg in range(8):
                    ensure_w(blk * 8 + fg + 2)
                    if fg + 1 < 8:
                        up(blk, fg + 1)
                    down(blk, fg)
                for tt in range(4):
                    final_tile(blk, tt)
            pg.barrier()
            pg.emit()
    return nc


def _t5_bucket(dist):
    nb, md = 32, 128
    me = nb // 2
    d = np.maximum(dist, 1).astype(np.float32)
    large = me + (np.log(d / np.float32(me)) / np.float32(math.log(md / me)) * np.float32(nb - me))
    large = np.minimum(large.astype(np.int32), nb - 1)
    return np.where(dist < me, dist, large)


_NC_CACHE = {}


def kernel(x, ln_attn_pre, w_in, b_f, lam_q1, lam_k1, lam_q2, lam_k2, subln_g, rel_bias,
           w_out, ln_attn_post, ln_mlp_pre, w_up, w_down, ln_mlp_post):
    f32 = np.float32
    debug = bool(int(os.environ.get("KDEBUG", "0")))
    x = np.asarray(x, f32)
    rel_bias = np.asarray(rel_bias, f32)

    def trep(g):
        g = np.asarray(g, f32).reshape(8, 128)
        return np.ascontiguousarray(np.broadcast_to(g.T[:, :, None], (128, 8, 128))).reshape(128, 1024)

    def brep(g):
        return np.ascontiguousarray(np.broadcast_to(np.asarray(g, f32).reshape(1, 1024), (128, 1024)))

    kl = np.arange(128)[:, None]
    u = np.arange(640)[None, :]
    dd = u - kl
    bidx = _t5_bucket(np.maximum(dd, 0))
    tl = np.empty((128, 5, 640), f32)
    for h in range(4):
        tl[:, h, :] = np.where(dd >= 0, rel_bias[bidx, h], f32(NEGBIG))
    tl[:, 4, :] = np.where(dd >= 0, f32(0.0), f32(NEGBIG))
    cvec = np.ascontiguousarray(np.broadcast_to(rel_bias[31].reshape(1, 4), (128, 4)))
    lamv = np.concatenate([np.asarray(a, f32).reshape(-1) for a in (lam_q1, lam_k1, lam_q2, lam_k2)])
    lamv = np.ascontiguousarray(np.broadcast_to(lamv.reshape(1, 256), (128, 256)))

    common = {
        "w_in": np.ascontiguousarray(np.asarray(w_in, f32)[0]),
        "w_out": np.ascontiguousarray(np.asarray(w_out, f32)[0]),
        "w_up": np.ascontiguousarray(np.asarray(w_up, f32)[0]),
        "w_down": np.ascontiguousarray(np.asarray(w_down, f32)[0]),
        "g_preT": trep(ln_attn_pre),
        "g_mlpT": trep(ln_mlp_pre),
        "g_post_b": brep(ln_attn_post),
        "g_mpost_b": brep(ln_mlp_post),
        "bf": np.asarray(b_f, f32).reshape(8, 1).copy(),
        "lamv": lamv,
        "subg": np.asarray(subln_g, f32).reshape(128, 1).copy(),
        "cvec": cvec,
        "tl": tl,
        "ident": np.eye(128, dtype=f32),
    }
    if debug not in _NC_CACHE:
        _NC_CACHE[debug] = build_program(debug)
    nc = _NC_CACHE[debug]
    n = 8
    in_maps = []
    for c in range(n):
        m = dict(common)
        m["x"] = np.ascontiguousarray(x[c * NSEQ:(c + 1) * NSEQ])
        in_maps.append(m)
    res = run_bass_kernel_spmd(nc, in_maps, core_ids=list(range(n)))
    if debug:
        kernel.last = res
    out = np.concatenate([np.asarray(r["out"]).reshape(NSEQ, S, D) for r in res.results], axis=0)
    return out.astype(f32, copy=False)
```

```python
import math
import os
from contextlib import ExitStack

import numpy as np
import concourse.bass as bass
import concourse.mybir as mybir
from concourse.bass_utils import run_bass_kernel_spmd

F32 = mybir.dt.float32
BF16 = mybir.dt.bfloat16
AF = mybir.ActivationFunctionType
ALU = mybir.AluOpType

ENGS = ("pe", "act", "dve", "pool", "sp")

S = 4096
D = 1024
NSEQ = 2
INC = 3080
DFF = 4096
EPS = 1e-6
NEGBIG = -1e30


class Op:
    __slots__ = ("eng", "fn", "reads", "writes", "dma", "track", "idx", "deps", "marked", "semval")

    def __init__(self, eng, fn, reads, writes, dma):
        self.eng = eng
        self.fn = fn
        self.reads = reads
        self.writes = writes
        self.dma = dma
        self.deps = []
        self.marked = False


class Prog:
    def __init__(self, nc, tag):
        self.nc = nc
        self.tag = tag
        self.ops = []
        self.state = {}
        self.track_ops = {}
        self.seen = {e: {} for e in ENGS}

    def _add(self, op):
        track = op.track
        lst = self.track_ops.setdefault(track, [])
        op.idx = len(lst)
        lst.append(op)
        deps = {}
        st = self.state
        tops = self.track_ops

        def need(t, i):
            if not isinstance(t, str) and t != track:
                tops[t][i].marked = True
            if deps.get(t, -1) < i:
                deps[t] = i

        for k in op.reads:
            s = st.get(k)
            if s is not None and s[0] is not None:
                need(*s[0])
        for k in op.writes:
            s = st.get(k)
            if s is not None:
                if s[0] is not None:
                    need(*s[0])
                for t, i in s[1].items():
                    need(t, i)
        me = (track, op.idx)
        for k in op.reads:
            s = st.get(k)
            if s is None:
                s = st[k] = [None, {}]
            s[1][track] = op.idx
        for k in op.writes:
            st[k] = [me, {}]
        seen = self.seen[op.eng]
        for t, i in deps.items():
            if t == track and (op.eng == "pe" or op.dma):
                continue
            if seen.get(t, -1) >= i:
                continue
            seen[t] = i
            op.deps.append((t, i))
            self.track_ops[t][i].marked = True
        self.ops.append(op)
        return op

    def op(self, eng, fn, reads=(), writes=()):
        o = Op(eng, fn, tuple(reads), tuple(writes), False)
        o.track = eng
        return self._add(o)

    def dma(self, queue, sem, fn, reads=(), writes=()):
        o = Op(queue, fn, tuple(reads), tuple(writes), True)
        o.track = ("dma", sem)
        return self._add(o)

    def barrier(self):
        for e in ENGS:
            deps = {}
            for k, s in self.state.items():
                if s[0] is not None:
                    t, i = s[0]
                    if not isinstance(t, str):
                        self.track_ops[t][i].marked = True
                    if deps.get(t, -1) < i:
                        deps[t] = i
                for t, i in s[1].items():
                    if not isinstance(t, str):
                        self.track_ops[t][i].marked = True
                    if deps.get(t, -1) < i:
                        deps[t] = i
            o = Op(e, None, (), (), False)
            o.track = e
            lst = self.track_ops.setdefault(e, [])
            o.idx = len(lst)
            lst.append(o)
            seen = self.seen[e]
            for t, i in deps.items():
                if t == e and e == "pe":
                    continue
                if seen.get(t, -1) >= i:
                    continue
                seen[t] = i
                o.deps.append((t, i))
                self.track_ops[t][i].marked = True
            self.ops.append(o)
        self.state = {}

    def emit(self):
        nc = self.nc
        for t, lst in self.track_ops.items():
            c = 0
            for o in lst:
                if o.marked:
                    c += 1
                o.semval = c
        with ExitStack() as es:
            sems = {}
            for t in self.track_ops:
                if any(o.marked for o in self.track_ops[t]):
                    nm = self.tag + "_" + (t if isinstance(t, str) else "d_" + str(t[1]))
                    sems[t] = es.enter_context(nc.semaphore(nm))
            block = es.enter_context(nc.Block())
            per_eng = {e: [o for o in self.ops if o.eng == e] for e in ENGS}
            track_ops = self.track_ops

            def run(engobj, ename):
                for o in per_eng[ename]:
                    for (t, i) in o.deps:
                        tgt = track_ops[t][i]
                        mult = 1 if isinstance(t, str) else 16
                        engobj.wait_ge(sems[t], tgt.semval * mult)
                    if o.fn is None:
                        continue
                    ins = o.fn(engobj)
                    if o.marked:
                        ins.then_inc(sems[o.track], 16 if o.dma else 1)

            @block.tensor
            def _(e):
                run(e, "pe")

            @block.scalar
            def _(e):
                run(e, "act")

            @block.vector
            def _(e):
                run(e, "dve")

            @block.gpsimd
            def _(e):
                run(e, "pool")

            @block.sync
            def _(e):
                run(e, "sp")


def build_program(debug=False):
    nc = bass.Bass("TRN2", target_bir_lowering=False)
    dk = "ExternalOutput" if debug else "Internal"

    def din(name, shape, dt=F32):
        return nc.dram_tensor(name, shape, dt, kind="ExternalInput").ap()

    x_d = din("x", [NSEQ, S, D])
    win_d = din("w_in", [D, INC])
    wout_d = din("w_out", [D, D])
    wup_d = din("w_up", [D, DFF])
    wdn_d = din("w_down", [DFF, D])
    gpreT_d = din("g_preT", [128, D])
    gmlpT_d = din("g_mlpT", [128, D])
    gpostb_d = din("g_post_b", [128, D])
    gmpostb_d = din("g_mpost_b", [128, D])
    bf_d = din("bf", [8, 1])
    lam_d = din("lamv", [128, 256])
    subg_d = din("subg", [128, 1])
    cvec_d = din("cvec", [128, 4])
    tl_d = din("tl", [128, 5, 640])
    ident_d = din("ident", [128, 128])
    out_d = nc.dram_tensor("out", [NSEQ, S, D], F32, kind="ExternalOutput").ap()

    qk_d = nc.dram_tensor("qk_s", [NSEQ, 2048, S], BF16, kind=dk).ap()
    aux_d = nc.dram_tensor("aux_s", [NSEQ, 8, S], BF16, kind=dk).ap()
    v_d = nc.dram_tensor("v_s", [NSEQ, S, 1024], BF16, kind=dk).ap()
    mix_d = nc.dram_tensor("mix_s", [NSEQ, 1024, S], BF16, kind=dk).ap()
    wupb_d = nc.dram_tensor("wup_bf", [D, DFF], BF16, kind="Internal").ap()
    wdnb_d = nc.dram_tensor("wdn_bf", [DFF, D], BF16, kind="Internal").ap()

    with ExitStack() as es0:
        def T0(name, shape, dt):
            return es0.enter_context(nc.sbuf_tensor(name, shape, dt))

        idt = T0("idt", [128, 128], F32)
        ones_bf = T0("ones_bf", [128, 128], BF16)
        ones_f = T0("ones_f", [128, 128], F32)
        neglam = T0("neglam", [128, 1], F32)
        g08 = T0("g08", [128, 1], F32)
        negD = T0("negD", [128, NSEQ * 256], F32)
        cvt = T0("cvt", [128, 4], F32)
        negbf = T0("negbf", [8, 1], F32)

        with ExitStack() as es:
            def T(name, shape, dt):
                return es.enter_context(nc.sbuf_tensor(name, shape, dt))
            lamt = T("lamt", [128, 256], F32)
            lprod = T("lprod", [128, 128], F32)
            lsum = T("lsum", [128, 2], F32)
            lexp = T("lexp", [128, 2], F32)
            subg = T("subg_t", [128, 1], F32)
            bft = T("bft", [8, 1], F32)
            pg = Prog(nc, "p0")
            pg.dma("sp", "a", lambda e: e.dma_start(out=idt[:], in_=ident_d), writes=["idt"])
            pg.dma("sp", "b", lambda e: e.dma_start(out=lamt[:], in_=lam_d), writes=["lamt"])
            pg.dma("sp", "c", lambda e: e.dma_start(out=subg[:], in_=subg_d), writes=["subg"])
            pg.dma("sp", "d", lambda e: e.dma_start(out=cvt[:], in_=cvec_d), writes=["cvt"])
            pg.dma("sp", "e", lambda e: e.dma_start(out=bft[:], in_=bf_d), writes=["bft"])
            pg.op("dve", lambda e: e.memset(ones_bf[:], 1.0), writes=["ones_bf"])
            pg.op("dve", lambda e: e.memset(ones_f[:], 1.0), writes=["ones_f"])
            pg.op("dve", lambda e: e.tensor_tensor(out=lprod[:, 0:64], in0=lamt[:, 0:64], in1=lamt[:, 64:128], op=ALU.mult),
                  reads=["lamt"], writes=["lp0"])
            pg.op("dve", lambda e: e.tensor_tensor(out=lprod[:, 64:128], in0=lamt[:, 128:192], in1=lamt[:, 192:256], op=ALU.mult),
                  reads=["lamt"], writes=["lp1"])
            pg.op("dve", lambda e: e.reduce_sum(out=lsum[:, 0:1], in_=lprod[:, 0:64], axis=mybir.AxisListType.X),
                  reads=["lp0"], writes=["ls0"])
            pg.op("dve", lambda e: e.reduce_sum(out=lsum[:, 1:2], in_=lprod[:, 64:128], axis=mybir.AxisListType.X),
                  reads=["lp1"], writes=["ls1"])
            pg.op("act", lambda e: e.activation(out=lexp[:], in_=lsum[:], func=AF.Exp), reads=["ls0", "ls1"], writes=["lexp"])
            pg.op("dve", lambda e: e.tensor_tensor(out=neglam[:], in0=lexp[:, 1:2], in1=lexp[:, 0:1], op=ALU.subtract),
                  reads=["lexp"], writes=["neglam"])
            pg.op("dve", lambda e: e.tensor_scalar(out=neglam[:], in0=neglam[:], scalar1=-0.2, scalar2=None, op0=ALU.add),
                  reads=["neglam"], writes=["neglam"])
            pg.op("dve", lambda e: e.tensor_scalar(out=g08[:], in0=subg[:], scalar1=0.8, scalar2=None, op0=ALU.mult),
                  reads=["subg"], writes=["g08"])
            pg.op("dve", lambda e: e.tensor_scalar(out=negbf[:], in0=bft[:], scalar1=-1.0, scalar2=None, op0=ALU.mult),
                  reads=["bft"], writes=["negbf"])
            pg.barrier()
            pg.emit()

        with ExitStack() as es:
            def T(name, shape, dt):
                return es.enter_context(nc.sbuf_tensor(name, shape, dt))

            def PS(name, shape, dt=F32):
                return es.enter_context(nc.psum_tensor(name, shape, dt))
            winb = T("winb", [128, 8, INC], BF16)
            gT = T("gT", [128, 8, 128], F32)
            xt = [T(f"xt{i}", [128, D], F32) for i in range(2)]
            xs = [T(f"xs{i}", [128, D], F32) for i in range(2)]
            sqj = T("sqj", [128, D], BF16)
            ss = [T(f"ss{i}", [128, 1], F32) for i in range(2)]
            lnv = [T(f"lnv{i}", [128, 1], F32) for i in range(2)]
            rs = [T(f"rs{i}", [128, 1], F32) for i in range(2)]
            xnT = [T(f"xnT{i}", [128, 8, 512], BF16) for i in range(2)]
            stQK = [T(f"stQK{i}", [128, 16, 512], BF16) for i in range(2)]
            stV = [T(f"stV{i}", [128, 4, 1024], BF16) for i in range(2)]
            e8 = T("e8", [8, 512], F32)
            sp8 = T("sp8", [8, 512], F32)
            ones8 = T("ones8", [8, 512], F32)
            cumT = T("cumT", [8, S], F32)
            cumb = T("cumb", [8, S], BF16)
            tp = [[PS(f"tp{i}{h}", [128, 512]) for h in range(2)] for i in range(2)]
            mm = [PS(f"mm{i}", [128, 512]) for i in range(3)]

            pg = Prog(nc, "pA")
            win_v = win_d.rearrange("(kc p) c -> p kc c", p=128)
            pg.dma("pool", "win0", lambda e: e.dma_start(out=winb[:, :, 0:1540], in_=win_v[:, :, 0:1540]), writes=["winb0"])
            pg.dma("pool", "win1", lambda e: e.dma_start(out=winb[:, :, 1540:INC], in_=win_v[:, :, 1540:INC]), writes=["winb1"])
            pg.dma("sp", "gT", lambda e: e.dma_start(out=gT[:], in_=gpreT_d.rearrange("p (k j) -> p k j", j=128)), writes=["gT"])
            pg.op("dve", lambda e: e.memset(ones8[:], 1.0), writes=["ones8"])
            wup_v = wup_d.rearrange("r (a c) -> (r a) c", c=2048)
            wupb_v = wupb_d.rearrange("r (a c) -> (r a) c", c=2048)
            for q in range(4):
                pg.dma("pool", f"wupc{q}", lambda e, q=q: e.dma_start(out=wupb_v[q * 512:(q + 1) * 512, :], in_=wup_v[q * 512:(q + 1) * 512, :]),
                       writes=[("wupb", q)])
            wdn_v = wdn_d.rearrange("(r a) c -> r (a c)", a=2)
            wdnb_v = wdnb_d.rearrange("(r a) c -> r (a c)", a=2)
            for q in range(4):
                pg.dma("pool", f"wdnc{q}", lambda e, q=q: e.dma_start(out=wdnb_v[q * 512:(q + 1) * 512, :], in_=wdn_v[q * 512:(q + 1) * 512, :]),
                       writes=[("wdnb", q)])

            def qk_cols(oc):
                if oc < 4:
                    return oc * 128, 0.125
                if oc < 8:
                    return 512 + (oc - 4) * 128, 1.0
                if oc < 12:
                    return 1536 + (oc - 8) * 128, 0.125
                return 2048 + (oc - 12) * 128, 1.0

            mmc = [0]

            def next_mm():
                b = mmc[0] % 3
                mmc[0] += 1
                return b

            evc = [0]
            for s in range(NSEQ):
                for tb in range(8):
                    blk = s * 8 + tb
                    xn = xnT[blk % 2]
                    xnk = ("xnT", blk % 2)
                    for tt in range(4):
                        ti = blk * 4 + tt
                        sl = ti % 2
                        r0 = tb * 512 + tt * 128
                        pg.dma("sp", f"x{sl}", lambda e, sl=sl, s=s, r0=r0: e.dma_start(out=xt[sl][:], in_=x_d[s, r0:r0 + 128, :]),
                               writes=[("xt", sl)])
                        pg.op("act", lambda e, sl=sl: e.activation(out=sqj[:], in_=xt[sl][:], func=AF.Square, accum_out=ss[sl][:]),
                              reads=[("xt", sl)], writes=[("ss", sl)])
                        pg.op("act", lambda e, sl=sl: e.activation(out=lnv[sl][:], in_=ss[sl][:], func=AF.Ln, scale=1.0 / D, bias=EPS),
                              reads=[("ss", sl)], writes=[("lnv", sl)])
                        pg.op("act", lambda e, sl=sl: e.activation(out=rs[sl][:], in_=lnv[sl][:], func=AF.Exp, scale=-0.5),
                              reads=[("lnv", sl)], writes=[("rs", sl)])
                        pg.op("act", lambda e, sl=sl: e.activation(out=xs[sl][:], in_=xt[sl][:], func=AF.Copy, scale=rs[sl][:]),
                              reads=[("xt", sl), ("rs", sl)], writes=[("xs", sl)])
                        for half in range(2):
                            for j in range(4):
                                kc = half * 4 + j
                                pg.op("pe", lambda e, sl=sl, half=half, j=j, kc=kc: e.transpose(
                                    out=tp[sl][half][:, j * 128:(j + 1) * 128], in_=xs[sl][:, kc * 128:(kc + 1) * 128], identity=idt[:]),
                                    reads=[("xs", sl), "idt"], writes=[("tp", sl, half)])
                            pg.op("dve", lambda e, sl=sl, half=half, xn=xn, tt=tt: e.tensor_tensor(
                                out=xn[:, half * 4:(half + 1) * 4, tt * 128:(tt + 1) * 128],
                                in0=tp[sl][half][:].rearrange("p (k j) -> p k j", j=128),
                                in1=gT[:, half * 4:(half + 1) * 4, :], op=ALU.mult),
                                reads=[("tp", sl, half), "gT"], writes=[xnk])
                    stq = stQK[blk % 2]
                    for oc in range(16):
                        c0, scl = qk_cols(oc)
                        b = next_mm()
                        for kc in range(8):
                            pg.op("pe", lambda e, b=b, kc=kc, c0=c0, xn=xn: e.matmul(
                                mm[b][:], lhsT=winb[:, kc, c0:c0 + 128], rhs=xn[:, kc, :], start=(kc == 0), stop=(kc == 7)),
                                reads=[xnk, "winb0", "winb1"], writes=[("mm", b)])
                        evc[0] += 1
                        if evc[0] % 2 == 0:
                            pg.op("act", lambda e, b=b, oc=oc, scl=scl, stq=stq: e.activation(
                                out=stq[:, oc, :], in_=mm[b][:], func=AF.Copy, scale=scl),
                                reads=[("mm", b)], writes=[("stQK", blk % 2)])
                        else:
                            pg.op("dve", lambda e, b=b, oc=oc, scl=scl, stq=stq: e.tensor_scalar(
                                out=stq[:, oc, :], in0=mm[b][:], scalar1=scl, scalar2=None, op0=ALU.mult),
                                reads=[("mm", b)], writes=[("stQK", blk % 2)])
                    pg.dma("pool", f"sqk{blk % 2}", lambda e, s=s, tb=tb, stq=stq: e.dma_start(
                        out=qk_d[s].rearrange("(oc p) t -> p oc t", p=128)[:, :, tb * 512:(tb + 1) * 512], in_=stq[:]),
                        reads=[("stQK", blk % 2)], writes=[("qk_d", s, tb)])
                    stv = stV[blk % 2]
                    for tt in range(4):
                        for half in range(2):
                            vc0 = 1024 if half == 0 else 2560
                            b = next_mm()
                            for kc in range(8):
                                pg.op("pe", lambda e, b=b, kc=kc, vc0=vc0, xn=xn, tt=tt: e.matmul(
                                    mm[b][:], lhsT=xn[:, kc, tt * 128:(tt + 1) * 128], rhs=winb[:, kc, vc0:vc0 + 512],
                                    start=(kc == 0), stop=(kc == 7)),
                                    reads=[xnk, "winb0", "winb1"], writes=[("mm", b)])
                            evc[0] += 1
                            if evc[0] % 2 == 0:
                                pg.op("act", lambda e, b=b, tt=tt, half=half, stv=stv: e.activation(
                                    out=stv[:, tt, half * 512:(half + 1) * 512], in_=mm[b][:], func=AF.Copy),
                                    reads=[("mm", b)], writes=[("stV", blk % 2)])
                            else:
                                pg.op("dve", lambda e, b=b, tt=tt, half=half, stv=stv: e.tensor_copy(
                                    out=stv[:, tt, half * 512:(half + 1) * 512], in_=mm[b][:]),
                                    reads=[("mm", b)], writes=[("stV", blk % 2)])
                    pg.dma("pool", f"sv{blk % 2}", lambda e, s=s, tb=tb, stv=stv: e.dma_start(
                        out=v_d[s, tb * 512:(tb + 1) * 512, :].rearrange("(tt p) c -> p tt c", p=128), in_=stv[:]),
                        reads=[("stV", blk % 2)], writes=[("v_d", s, tb)])
                    b = next_mm()
                    for kc in range(8):
                        pg.op("pe", lambda e, b=b, kc=kc, xn=xn: e.matmul(
                            mm[b][0:8, :], lhsT=winb[:, kc, 3072:3080], rhs=xn[:, kc, :], start=(kc == 0), stop=(kc == 7)),
                            reads=[xnk, "winb0", "winb1"], writes=[("mm", b)])
                    pg.op("act", lambda e, b=b: e.activation(out=e8[:], in_=mm[b][0:8, :], func=AF.Exp, scale=-1.0, bias=negbf[:]),
                          reads=[("mm", b), "negbf"], writes=["e8"])
                    pg.op("act", lambda e: e.activation(out=sp8[:], in_=e8[:], func=AF.Ln, bias=1.0),
                          reads=["e8"], writes=["sp8"])
                    if tb == 0:
                        pg.op("dve", lambda e: e.tensor_tensor_scan(out=cumT[:, 0:512], data0=ones8[:], data1=sp8[:], initial=0.0,
                                                                    op0=ALU.mult, op1=ALU.subtract),
                              reads=["sp8", "ones8"], writes=["cumT"])
                    else:
                        pg.op("dve", lambda e, tb=tb: e.tensor_tensor_scan(
                            out=cumT[:, tb * 512:(tb + 1) * 512], data0=ones8[:], data1=sp8[:],
                            initial=cumT[:, tb * 512 - 1:tb * 512], op0=ALU.mult, op1=ALU.subtract),
                            reads=["sp8", "ones8", "cumT"], writes=["cumT"])
                pg.op("dve", lambda e: e.tensor_copy(out=cumb[:], in_=cumT[:]), reads=["cumT"], writes=["cumb"])
                pg.dma("sp", "aux", lambda e, s=s: e.dma_start(out=aux_d[s], in_=cumb[:]), reads=["cumb"], writes=[("aux_d", s)])
                b = next_mm()
                for kb in range(32):
                    pg.op("pe", lambda e, b=b, kb=kb: e.transpose(
                        out=mm[b][:, kb * 8:(kb + 1) * 8], in_=cumT[0:8, kb * 128:(kb + 1) * 128], identity=idt[0:8, 0:8]),
                        reads=["cumT", "idt"], writes=[("mm", b)])
                pg.op("dve", lambda e, b=b, s=s: e.tensor_scalar(
                    out=negD[:, s * 256:(s + 1) * 256], in0=mm[b][:, 0:256], scalar1=-1.0, scalar2=None, op0=ALU.mult),
                    reads=[("mm", b)], writes=["negD"])
            pg.barrier()
            pg.emit()

        with ExitStack() as es:
            def T(name, shape, dt):
                return es.enter_context(nc.sbuf_tensor(name, shape, dt))

            def PS(name, shape, dt=F32):
                return es.enter_context(nc.psum_tensor(name, shape, dt))
            QT = [T(f"QT{i}", [128, S], BF16) for i in range(2)]
            QT2 = [T(f"QT2{i}", [128, S], BF16) for i in range(2)]
            KT = [T(f"KT{i}", [128, S], BF16) for i in range(2)]
            VT = [T(f"VT{i}", [128, 32, 128], BF16) for i in range(2)]
            TL = T("TL", [128, 5, 640], F32)
            NPT = 6
            NTB = 4
            PT = [T(f"PT{i}", [128, 512], BF16) for i in range(NPT)]
            tmpb = [T(f"tmpb{i}", [128, 512], F32) for i in range(NTB)]
            mix = [T(f"mix{i}", [128, S], BF16) for i in range(2)]
            R0 = T("R0", [128, 512], F32)
            R1 = T("R1", [128, 512], F32)
            Tt0 = T("Tt0", [128, 512], F32)
            Tt1 = T("Tt1", [128, 512], F32)
            od = T("od", [128, 512], F32)
            sqd = T("sqd", [128, 512], F32)
            lnd = T("lnd", [128, 512], F32)
            rsd = T("rsd", [128, 512], F32)
            NSB = 4
            Sb = [PS(f"Sb{i}", [128, 512]) for i in range(NSB)]
            Ob = [PS(f"Ob{i}", [128, 512]) for i in range(2)]
            Lb = [PS(f"Lb{i}", [128, 512]) for i in range(2)]

            pg = Prog(nc, "pB")
            pg.dma("sp", "tl", lambda e: e.dma_start(out=TL[:], in_=tl_d), writes=["TL"])

            units = []
            for s in range(NSEQ):
                for h in range(4):
                    units.append((s, "d", h))
                for h in range(8):
                    units.append((s, "f", h))

            def load_unit(ui):
                s, kind, h = units[ui]
                sl = ui % 2
                if kind == "d":
                    r = h * 128
                    pg.op("pool", lambda e: e.memset(QT[sl][64:128, :], 0.0), writes=[("Qhi", sl)])
                    pg.op("pool", lambda e: e.memset(QT2[sl][0:64, :], 0.0), writes=[("Q2lo", sl)])
                    pg.dma("sp", f"q{sl}", lambda e: e.dma_start(out=QT[sl][0:64, :], in_=qk_d[s, r:r + 64, :]),
                           writes=[("Qlo", sl)])
                    pg.dma("sp", f"q{sl}", lambda e: e.dma_start(out=QT2[sl][64:128, :], in_=qk_d[s, r + 64:r + 128, :]),
                           writes=[("Q2hi", sl)])
                    pg.dma("sp", f"k{sl}", lambda e: e.dma_start(out=KT[sl][:], in_=qk_d[s, (4 + h) * 128:(5 + h) * 128, :]),
                           writes=[("Klo", sl), ("Khi", sl)])
                    pg.dma("sp", f"v{sl}", lambda e: e.dma_start(
                        out=VT[sl][:], in_=v_d[s, :, h * 128:(h + 1) * 128].rearrange("(kb p) c -> p kb c", p=128)),
                        writes=[("Vlo", sl), ("Vhi", sl)])
                else:
                    qr = (8 + h // 2) * 128 + (h % 2) * 64
                    kr = (12 + h // 2) * 128 + (h % 2) * 64
                    po = (h % 2) * 64
                    oth = 64 - po
                    vk = ("Vlo", sl) if po == 0 else ("Vhi", sl)
                    vok = ("Vhi", sl) if po == 0 else ("Vlo", sl)
                    pg.op("pool", lambda e: e.memset(KT[sl][64:65, :], 1.0), writes=[("Khi", sl)])
                    pg.op("pool", lambda e: e.memset(VT[sl][:, :, oth:oth + 64], 0.0), writes=[vok])
                    pg.dma("sp", f"q{sl}", lambda e: e.dma_start(out=QT[sl][0:64, :], in_=qk_d[s, qr:qr + 64, :]),
                           writes=[("Qlo", sl)])
                    pg.dma("sp", f"q{sl}", lambda e: e.dma_start(out=QT[sl][64:65, :], in_=aux_d[s, h:h + 1, :]),
                           writes=[("Qhi", sl)])
                    pg.dma("sp", f"k{sl}", lambda e: e.dma_start(out=KT[sl][0:64, :], in_=qk_d[s, kr:kr + 64, :]),
                           writes=[("Klo", sl)])
                    pg.dma("sp", f"v{sl}", lambda e: e.dma_start(
                        out=VT[sl][:, :, po:po + 64], in_=v_d[s, :, 512 + h * 64:512 + (h + 1) * 64].rearrange("(kb p) c -> p kb c", p=128)),
                        writes=[vk])

            sctr = [0]
            pctr = [0]
            tctr = [0]
            G = 2

            def run_unit(ui):
                s, kind, h = units[ui]
                sl = ui % 2
                nstream = 2 if kind == "d" else 1
                if kind == "d":
                    chunk = h
                    po = 0
                    tli = h
                else:
                    chunk = 4 + h // 2
                    po = (h % 2) * 64
                    tli = 4
                csl = chunk % 2
                kkeys = [("Klo", sl), ("Khi", sl)]
                qkeys = [[("Qlo", sl), ("Qhi", sl)], [("Q2lo", sl), ("Q2hi", sl)]]
                vkeys = [("Vlo", sl), ("Vhi", sl)]
                blocks = []
                for qb in range(8):
                    nkb = 4 * qb + 4
                    for i in range(nstream):
                        for kb in range(nkb):
                            blocks.append((qb, kb, i, kb == nkb - 1))
                info = {}
                pending = []
                cur_batch = [0]

                def acc_idx(qb, i):
                    return i if kind == "d" else (qb % 2)

                def emit_qk(n):
                    qb, kb, i, _ = blocks[n]
                    off = qb * 512 - kb * 128
                    c0 = max(0, -off)
                    sb = sctr[0] % NSB
                    sctr[0] += 1
                    ps = pctr[0] % NPT
                    pctr[0] += 1
                    info[n] = (sb, c0, off, ps)
                    if kind == "d":
                        pr = slice(0, 128)
                        qt = QT[sl] if i == 0 else QT2[sl]
                    else:
                        pr = slice(0, 65)
                        qt = QT[sl]
                    pg.op("pe", lambda e: e.matmul(
                        Sb[sb][:, c0:512], lhsT=KT[sl][pr, kb * 128:(kb + 1) * 128],
                        rhs=qt[pr, qb * 512 + c0:(qb + 1) * 512], start=True, stop=True),
                        reads=qkeys[i] + kkeys, writes=[("S", sb)])

                def emit_act(n):
                    qb, kb, i, lastq = blocks[n]
                    sb, c0, off, ps = info[n]
                    special = (off <= 128) if kind == "d" else (off <= 0)
                    if kind == "d":
                        bias_ap = None if special else cvt[:, h:h + 1]
                    else:
                        ix = (s * 32 + kb) * 8 + h
                        bias_ap = negD[:, ix:ix + 1]
                    if special:
                        ts = tctr[0] % NTB
                        tctr[0] += 1
                        t0 = max(off, 0)
                        pg.op("dve", lambda e: e.tensor_tensor(
                            out=tmpb[ts][:, c0:512], in0=Sb[sb][:, c0:512], in1=TL[:, tli, t0:t0 + 512 - c0], op=ALU.add),
                            reads=[("S", sb), "TL"], writes=[("tmp", ts)])
                        src = tmpb[ts]
                        rk = ("tmp", ts)
                    else:
                        src = Sb[sb]
                        rk = ("S", sb)
                    if bias_ap is None:
                        pg.op("act", lambda e: e.activation(out=PT[ps][:, c0:512], in_=src[:, c0:512], func=AF.Exp),
                              reads=[rk], writes=[("PT", ps)])
                    else:
                        pg.op("act", lambda e: e.activation(out=PT[ps][:, c0:512], in_=src[:, c0:512], func=AF.Exp, bias=bias_ap),
                              reads=[rk, "negD", "cvt"], writes=[("PT", ps)])

                def emit_pv(n):
                    qb, kb, i, lastq = blocks[n]
                    sb, c0, off, ps = info.pop(n)
                    nkb = 4 * qb + 4
                    a = acc_idx(qb, i)
                    pg.op("pe", lambda e: e.matmul(
                        Ob[a][:, c0:512], lhsT=VT[sl][:, kb, :], rhs=PT[ps][:, c0:512],
                        start=(kb == 0), stop=(kb == nkb - 1)),
                        reads=vkeys + [("PT", ps)], writes=[("O", a)])
                    pg.op("pe", lambda e: e.matmul(
                        Lb[a][:, c0:512], lhsT=ones_bf[:, :], rhs=PT[ps][:, c0:512],
                        start=(kb == 0), stop=(kb == nkb - 1)),
                        reads=["ones_bf", ("PT", ps)], writes=[("L", a)])
                    if lastq:
                        finalize(qb, i)

                def finalize(qb, i):
                    cs = slice(qb * 512, (qb + 1) * 512)
                    mk = ("mix", csl)
                    if kind == "d":
                        if i == 0:
                            pg.op("dve", lambda e: e.reciprocal(out=R0[:], in_=Lb[0][:]), reads=[("L", 0)], writes=["R0"])
                            pg.op("dve", lambda e: e.tensor_tensor(out=Tt0[:], in0=Ob[0][:], in1=R0[:], op=ALU.mult),
                                  reads=[("O", 0), "R0"], writes=["T0"])
                            return
                        pg.op("dve", lambda e: e.reciprocal(out=R1[:], in_=Lb[1][:]), reads=[("L", 1)], writes=["R1"])
                        pg.op("dve", lambda e: e.tensor_tensor(out=Tt1[:], in0=Ob[1][:], in1=R1[:], op=ALU.mult),
                              reads=[("O", 1), "R1"], writes=["T1"])
                        pg.op("dve", lambda e: e.scalar_tensor_tensor(
                            out=od[:], in0=Tt1[:], scalar=neglam[:, 0:1], in1=Tt0[:], op0=ALU.mult, op1=ALU.add),
                            reads=["T0", "T1"], writes=["od"])
                        pg.op("pool", lambda e: e.tensor_tensor(out=sqd[:], in0=od[:], in1=od[:], op=ALU.mult),
                              reads=["od"], writes=["sqd"])

                        def part2():
                            pg.op("pe", lambda e: e.matmul(Lb[1][:], lhsT=ones_f[:], rhs=sqd[:], start=True, stop=True),
                                  reads=["sqd", "ones_f"], writes=[("L", 1)])
                            pg.op("act", lambda e: e.activation(out=lnd[:], in_=Lb[1][:], func=AF.Ln, scale=1.0 / 128, bias=EPS),
                                  reads=[("L", 1)], writes=["lnd"])
                            pg.op("act", lambda e: e.activation(out=rsd[:], in_=lnd[:], func=AF.Exp, scale=-0.5),
                                  reads=["lnd"], writes=["rsd"])
                            pg.op("dve", lambda e: e.scalar_tensor_tensor(
                                out=mix[csl][:, cs], in0=od[:], scalar=g08[:, 0:1], in1=rsd[:], op0=ALU.mult, op1=ALU.mult),
                                reads=["od", "rsd"], writes=[mk])
                        pending.append((cur_batch[0] + 3, part2))
                    else:
                        a = acc_idx(qb, i)
                        pp = slice(po, po + 64)
                        pg.op("dve", lambda e: e.reciprocal(out=R0[pp, :], in_=Lb[a][pp, :]), reads=[("L", a)], writes=["R0"])
                        pg.op("dve", lambda e: e.tensor_tensor(out=mix[csl][pp, cs], in0=Ob[a][pp, :], in1=R0[pp, :], op=ALU.mult),
                              reads=[("O", a), "R0"], writes=[mk])

                def run_pending(force=False):
                    keep = []
                    for due, fn in pending:
                        if force or due <= cur_batch[0]:
                            fn()
                        else:
                            keep.append((due, fn))
                    pending[:] = keep

                nb = len(blocks)
                batches = [list(range(j, min(j + G, nb))) for j in range(0, nb, G)]
                for n in batches[0]:
                    emit_qk(n)
                for bi, bt in enumerate(batches):
                    cur_batch[0] = bi
                    if bi + 1 < len(batches):
                        for n in batches[bi + 1]:
                            emit_qk(n)
                    for n in bt:
                        emit_act(n)
                    for n in bt:
                        emit_pv(n)
                    run_pending()
                run_pending(force=True)
                if kind == "d" or (h % 2 == 1):
                    pg.dma("sp", f"mx{csl}", lambda e: e.dma_start(out=mix_d[s, chunk * 128:(chunk + 1) * 128, :], in_=mix[csl][:]),
                           reads=[("mix", csl)], writes=[("mix_d", s, chunk)])

            load_unit(0)
            for ui in range(len(units)):
                if ui + 1 < len(units):
                    load_unit(ui + 1)
                run_unit(ui)
            pg.barrier()
            pg.emit()

        with ExitStack() as es:
            def T(name, shape, dt):
                return es.enter_context(nc.sbuf_tensor(name, shape, dt))

            def PS(name, shape, dt=F32):
                return es.enter_context(nc.psum_tensor(name, shape, dt))
            woutb = T("woutb", [128, 8, D], BF16)
            gpostb = T("gpostb", [128, D], F32)
            gmpostb = T("gmpostb", [128, D], F32)
            g2T = T("g2T", [128, 8, 128], F32)
            mixb = [T(f"mixb{i}", [128, 8, 512], BF16) for i in range(2)]
            xt = [T(f"cxt{i}", [128, D], F32) for i in range(2)]
            x1b = [T(f"x1_{i}", [128, 4, D], F32) for i in range(2)]
            xs2 = [T(f"xs2{i}", [128, D], F32) for i in range(2)]
            sqj = T("csqj", [128, D], BF16)
            h2T = [T(f"h2T{i}", [128, 8, 512], BF16) for i in range(2)]
            wup = [T(f"wup{i}", [128, 8, 512], BF16) for i in range(3)]
            wdn = [T(f"wdn{i}", [128, 4, D], BF16) for i in range(3)]
            rl = [T(f"rl{i}", [128, 512], F32) for i in range(2)]
            uT = [T(f"uT{i}", [128, 4, 512], BF16) for i in range(2)]
            acc = T("acc", [128, 4, D], F32)
            outst = [T(f"outst{i}", [128, D], F32) for i in range(2)]
            ssc = [T(f"ssc{i}", [128, 1], F32) for i in range(6)]
            lnc = [T(f"lnc{i}", [128, 1], F32) for i in range(6)]
            rsc = [T(f"rsc{i}", [128, 1], F32) for i in range(6)]
            Y = PS("Y", [128, 1024])
            TP = [PS(f"TP{i}", [128, 512]) for i in range(2)]
            U = [PS(f"U{i}", [128, 512]) for i in range(2)]
            Yd = [PS(f"Yd{i}", [128, 512]) for i in range(2)]

            pg = Prog(nc, "pC")
            pg.dma("pool", "wo", lambda e: e.dma_start(out=woutb[:], in_=wout_d.rearrange("(kc p) c -> p kc c", p=128)), writes=["woutb"])
            pg.dma("sp", "g1", lambda e: e.dma_start(out=gpostb[:], in_=gpostb_d), writes=["gpostb"])
            pg.dma("sp", "g2", lambda e: e.dma_start(out=gmpostb[:], in_=gmpostb_d), writes=["gmpostb"])
            pg.dma("sp", "g3", lambda e: e.dma_start(out=g2T[:], in_=gmlpT_d.rearrange("p (k j) -> p k j", j=128)), writes=["g2T"])

            nblk = NSEQ * 8
            nrm = [0]

            def rstd_from(src_ap, srckeys, n_el):
                k = nrm[0] % 6
                nrm[0] += 1
                pg.op("act", lambda e: e.activation(out=sqj[:], in_=src_ap, func=AF.Square, accum_out=ssc[k][:]),
                      reads=srckeys, writes=[("ssc", k)])
                pg.op("act", lambda e: e.activation(out=lnc[k][:], in_=ssc[k][:], func=AF.Ln, scale=1.0 / n_el, bias=EPS),
                      reads=[("ssc", k)], writes=[("lnc", k)])
                pg.op("act", lambda e: e.activation(out=rsc[k][:], in_=lnc[k][:], func=AF.Exp, scale=-0.5),
                      reads=[("lnc", k)], writes=[("rsc", k)])
                return rsc[k], ("rsc", k)

            NW = 3
            witems = [(blk, fg) for blk in range(nblk) for fg in range(8)]
            wnext = [0]

            def ensure_w(upto):
                while wnext[0] <= upto and wnext[0] < len(witems):
                    load_w(wnext[0])
                    wnext[0] += 1

            def load_w(item):
                blk, fg = witems[item]
                i = item % NW
                pg.dma("sp", f"wu{i}", lambda e: e.dma_start(
                    out=wup[i][:], in_=wupb_d.rearrange("(kc p) c -> p kc c", p=128)[:, :, fg * 512:(fg + 1) * 512]),
                    reads=[("wupb", q) for q in range(4)], writes=[("wup", i)])
                pg.dma("sp", f"wd{i}", lambda e: e.dma_start(
                    out=wdn[i][:], in_=wdnb_d[fg * 512:(fg + 1) * 512, :].rearrange("(fc p) c -> p fc c", p=128)),
                    reads=[("wdnb", q) for q in range(4)], writes=[("wdn", i)])

            def load_mix(blk):
                s, tb = divmod(blk, 8)
                i = blk % 2
                pg.dma("sp", f"mb{i}", lambda e: e.dma_start(
                    out=mixb[i][:], in_=mix_d[s].rearrange("(kc p) t -> p kc t", p=128)[:, :, tb * 512:(tb + 1) * 512]),
                    writes=[("mixb", i)])

            xl = [0]
            ost = [0]

            cstate = {}

            def c1(blk, tt):
                s, tb = divmod(blk, 8)
                mb = mixb[blk % 2]
                x1 = x1b[blk % 2]
                xk = ("x1", blk % 2, tt)
                r0 = tb * 512 + tt * 128
                xi = xl[0] % 2
                xl[0] += 1
                pg.dma("sp", f"cx{xi}", lambda e: e.dma_start(out=xt[xi][:], in_=x_d[s, r0:r0 + 128, :]),
                       writes=[("xt", xi)])
                for half in range(2):
                    for kc in range(8):
                        pg.op("pe", lambda e, half=half, kc=kc: e.matmul(
                            Y[:, half * 512:(half + 1) * 512], lhsT=mb[:, kc, tt * 128:(tt + 1) * 128],
                            rhs=woutb[:, kc, half * 512:(half + 1) * 512], start=(kc == 0), stop=(kc == 7)),
                            reads=[("mixb", blk % 2), "woutb"], writes=["Y"])
                rt, rk = rstd_from(Y[:], ["Y"], D)
                pg.op("dve", lambda e: e.scalar_tensor_tensor(
                    out=x1[:, tt, :], in0=Y[:], scalar=rt[:, 0:1], in1=gpostb[:], op0=ALU.mult, op1=ALU.mult),
                    reads=["Y", rk, "gpostb"], writes=[xk])
                pg.op("dve", lambda e: e.tensor_tensor(out=x1[:, tt, :], in0=x1[:, tt, :], in1=xt[xi][:], op=ALU.add),
                      reads=[xk, ("xt", xi)], writes=[xk])
                rt2, rk2 = rstd_from(x1[:, tt, :], [xk], D)
                xsi = xi
                pg.op("act", lambda e: e.activation(out=xs2[xsi][:], in_=x1[:, tt, :], func=AF.Copy, scale=rt2[:]),
                      reads=[xk, rk2], writes=[("xs2", xsi)])
                cstate[(blk, tt)] = xsi

            def c2(blk, tt):
                hT = h2T[blk % 2]
                hk = ("h2T", blk % 2)
                xsi = cstate.pop((blk, tt))
                for half in range(2):
                    for j in range(4):
                        kc = half * 4 + j
                        pg.op("pe", lambda e, half=half, j=j, kc=kc: e.transpose(
                            out=TP[half][:, j * 128:(j + 1) * 128], in_=xs2[xsi][:, kc * 128:(kc + 1) * 128], identity=idt[:]),
                            reads=[("xs2", xsi), "idt"], writes=[("TP", half)])
                    pg.op("dve", lambda e, half=half: e.tensor_tensor(
                        out=hT[:, half * 4:(half + 1) * 4, tt * 128:(tt + 1) * 128],
                        in0=TP[half][:].rearrange("p (k j) -> p k j", j=128),
                        in1=g2T[:, half * 4:(half + 1) * 4, :], op=ALU.mult),
                        reads=[("TP", half), "g2T"], writes=[hk])

            def up(blk, fg):
                hT = h2T[blk % 2]
                hk = ("h2T", blk % 2)
                wi = (blk * 8 + fg) % NW
                ui_ = fg % 2
                for fcl in range(4):
                    ub = (fg * 4 + fcl) % 2
                    for kc in range(8):
                        pg.op("pe", lambda e, ub=ub, kc=kc, fcl=fcl: e.matmul(
                            U[ub][:], lhsT=wup[wi][:, kc, fcl * 128:(fcl + 1) * 128], rhs=hT[:, kc, :],
                            start=(kc == 0), stop=(kc == 7)),
                            reads=[("wup", wi), hk], writes=[("U", ub)])
                    pg.op("act", lambda e, ub=ub: e.activation(out=rl[ub][:], in_=U[ub][:], func=AF.Relu),
                          reads=[("U", ub)], writes=[("rl", ub)])
                    pg.op("pool", lambda e, ub=ub, fcl=fcl: e.tensor_tensor(
                        out=uT[ui_][:, fcl, :], in0=rl[ub][:], in1=rl[ub][:], op=ALU.mult),
                        reads=[("rl", ub)], writes=[("uT", ui_)])

            def down(blk, fg):
                wi = (blk * 8 + fg) % NW
                ui_ = fg % 2
                for tt in range(4):
                    for half in range(2):
                        yb = half
                        for fcl in range(4):
                            pg.op("pe", lambda e, yb=yb, fcl=fcl, tt=tt, half=half: e.matmul(
                                Yd[yb][:], lhsT=uT[ui_][:, fcl, tt * 128:(tt + 1) * 128],
                                rhs=wdn[wi][:, fcl, half * 512:(half + 1) * 512], start=(fcl == 0), stop=(fcl == 3)),
                                reads=[("uT", ui_), ("wdn", wi)], writes=[("Yd", yb)])
                        ak = ("acc", tt, half)
                        if fg == 0:
                            pg.op("dve", lambda e, yb=yb, tt=tt, half=half: e.tensor_copy(
                                out=acc[:, tt, half * 512:(half + 1) * 512], in_=Yd[yb][:]),
                                reads=[("Yd", yb)], writes=[ak])
                        else:
                            pg.op("dve", lambda e, yb=yb, tt=tt, half=half: e.tensor_tensor(
                                out=acc[:, tt, half * 512:(half + 1) * 512], in0=Yd[yb][:],
                                in1=acc[:, tt, half * 512:(half + 1) * 512], op=ALU.add),
                                reads=[("Yd", yb), ak], writes=[ak])

            def final_tile(blk, tt):
                s, tb = divmod(blk, 8)
                r0 = tb * 512 + tt * 128
                rt3, rk3 = rstd_from(acc[:, tt, :], [("acc", tt, 0), ("acc", tt, 1)], D)
                oi = ost[0] % 2
                ost[0] += 1
                pg.op("dve", lambda e: e.scalar_tensor_tensor(
                    out=outst[oi][:], in0=acc[:, tt, :], scalar=rt3[:, 0:1], in1=gmpostb[:], op0=ALU.mult, op1=ALU.mult),
                    reads=[("acc", tt, 0), ("acc", tt, 1), rk3, "gmpostb"], writes=[("ost", oi)])
                x1 = x1b[blk % 2]
                pg.op("pool", lambda e: e.tensor_tensor(out=outst[oi][:], in0=outst[oi][:], in1=x1[:, tt, :], op=ALU.add),
                      reads=[("ost", oi), ("x1", blk % 2, tt)], writes=[("ost", oi)])
                pg.dma("pool", f"o{oi}", lambda e: e.dma_start(out=out_d[s, r0:r0 + 128, :], in_=outst[oi][:]),
                       reads=[("ost", oi)], writes=[("out", blk, tt)])

            load_mix(0)
            ensure_w(1)
            for tt in range(4):
                c1(0, tt)
                c2(0, tt)
            up(0, 0)
            for blk in range(nblk):
                nxt = blk + 1 < nblk
                if nxt:
                    load_mix(blk + 1)
                for fg in range(8):
                    ensure_w(blk * 8 + fg + 2)
                    if fg < 7:
                        up(blk, fg + 1)
                    if nxt:
                        if fg % 2 == 0:
                            c1(blk + 1, fg // 2)
                        else:
                            c2(blk + 1, fg // 2)
                    down(blk, fg)
                    if fg == 7 and nxt:
                        up(blk + 1, 0)
                for tt in range(4):
                    final_tile(blk, tt)
            pg.barrier()
            pg.emit()
    return nc


def _t5_bucket(dist):
    nb, md = 32, 128
    me = nb // 2
    d = np.maximum(dist, 1).astype(np.float32)
    large = me + (np.log(d / np.float32(me)) / np.float32(math.log(md / me)) * np.float32(nb - me))
    large = np.minimum(large.astype(np.int32), nb - 1)
    return np.where(dist < me, dist, large)


_NC_CACHE = {}


def kernel(x, ln_attn_pre, w_in, b_f, lam_q1, lam_k1, lam_q2, lam_k2, subln_g, rel_bias,
           w_out, ln_attn_post, ln_mlp_pre, w_up, w_down, ln_mlp_post):
    f32 = np.float32
    debug = bool(int(os.environ.get("KDEBUG", "0")))
    x = np.asarray(x, f32)
    rel_bias = np.asarray(rel_bias, f32)

    def trep(g):
        g = np.asarray(g, f32).reshape(8, 128)
        return np.ascontiguousarray(np.broadcast_to(g.T[:, :, None], (128, 8, 128))).reshape(128, 1024)

    def brep(g):
        return np.ascontiguousarray(np.broadcast_to(np.asarray(g, f32).reshape(1, 1024), (128, 1024)))

    kl = np.arange(128)[:, None]
    u = np.arange(640)[None, :]
    dd = u - kl
    bidx = _t5_bucket(np.maximum(dd, 0))
    tl = np.empty((128, 5, 640), f32)
    for h in range(4):
        tl[:, h, :] = np.where(dd >= 0, rel_bias[bidx, h], f32(NEGBIG))
    tl[:, 4, :] = np.where(dd >= 0, f32(0.0), f32(NEGBIG))
    cvec = np.ascontiguousarray(np.broadcast_to(rel_bias[31].reshape(1, 4), (128, 4)))
    lamv = np.concatenate([np.asarray(a, f32).reshape(-1) for a in (lam_q1, lam_k1, lam_q2, lam_k2)])
    lamv = np.ascontiguousarray(np.broadcast_to(lamv.reshape(1, 256), (128, 256)))

    common = {
        "w_in": np.ascontiguousarray(np.asarray(w_in, f32)[0]),
        "w_out": np.ascontiguousarray(np.asarray(w_out, f32)[0]),
        "w_up": np.ascontiguousarray(np.asarray(w_up, f32)[0]),
        "w_down": np.ascontiguousarray(np.asarray(w_down, f32)[0]),
        "g_preT": trep(ln_attn_pre),
        "g_mlpT": trep(ln_mlp_pre),
        "g_post_b": brep(ln_attn_post),
        "g_mpost_b": brep(ln_mlp_post),
        "bf": np.asarray(b_f, f32).reshape(8, 1).copy(),
        "lamv": lamv,
        "subg": np.asarray(subln_g, f32).reshape(128, 1).copy(),
        "cvec": cvec,
        "tl": tl,
        "ident": np.eye(128, dtype=f32),
    }
    if debug not in _NC_CACHE:
        _NC_CACHE[debug] = build_program(debug)
    nc = _NC_CACHE[debug]
    n = 8
    in_maps = []
    for c in range(n):
        m = dict(common)
        m["x"] = np.ascontiguousarray(x[c * NSEQ:(c + 1) * NSEQ])
        in_maps.append(m)
    res = run_bass_kernel_spmd(nc, in_maps, core_ids=list(range(n)))
    if debug:
        kernel.last = res
    out = np.concatenate([np.asarray(r["out"]).reshape(NSEQ, S, D) for r in res.results], axis=0)
    return out.astype(f32, copy=False)
```
